# Optimizing a Trainium2 kernel written in Bass

```python
import jax, jax.numpy as jnp
from jax import lax
import numpy as np

D_MODEL = 2048
BATCH = 4
SEQ = 4096
DEPTH = 1

A_HEAD_DIM = 128
A_WIDTH = D_MODEL // 2
A_HEADS = A_WIDTH // A_HEAD_DIM
A_CHUNK = 64
B_HEAD_DIM = 64
B_WIDTH = D_MODEL // 2
B_Q_HEADS = B_WIDTH // B_HEAD_DIM
B_GROUP = 4
B_KV_HEADS = B_Q_HEADS // B_GROUP
B_KV_WIDTH = B_KV_HEADS * B_HEAD_DIM
WINDOW = 128
BLOCK = 128
MLP_HIDDEN = 4 * D_MODEL
N_MOD = 6
EPS = 1e-6

SPLIT_SIZES = (A_WIDTH, A_WIDTH, A_WIDTH, A_WIDTH,
               B_WIDTH, B_KV_WIDTH, B_KV_WIDTH,
               D_MODEL, D_MODEL)
IN_WIDTH = 4 * A_WIDTH + B_WIDTH + 2 * B_KV_WIDTH + 2 * D_MODEL

kernel_name = "hybrid_hgrn2_swa_sink_gated_block"


def split_columns(t):
    idx, acc = [], 0
    for s in SPLIT_SIZES[:-1]:
        acc += s
        idx.append(acc)
    return jnp.split(t, idx, axis=-1)


def rms_norm(x, gain):
    xf = x.astype(jnp.float32)
    y = xf * lax.rsqrt(jnp.mean(xf * xf, axis=-1, keepdims=True) + EPS)
    return (y * gain.astype(jnp.float32)).astype(x.dtype)


def head_rms(t, gain):
    return t * lax.rsqrt(jnp.mean(t * t, axis=-1, keepdims=True) + EPS) * gain.astype(jnp.float32)


def hgrn2_mixer(q, f_logit, i, g, lb, o_gain):
    f32 = jnp.float32
    bsz, seq, _ = q.shape
    H, K, C = A_HEADS, A_HEAD_DIM, A_CHUNK
    n = seq // C
    lbf = lb.astype(f32)
    f = lbf + (1.0 - lbf) * jax.nn.sigmoid(f_logit.astype(f32))
    log_f = jnp.log(f)
    k = 1.0 - f
    qf = jax.nn.silu(q.astype(f32))

    def to_chunks(t):
        return t.reshape(bsz, n, C, H, K).transpose(0, 3, 1, 2, 4)

    qc, kc, vc, lfc = (to_chunks(t) for t in (qf, k, i.astype(f32), log_f))
    b = jnp.cumsum(lfc, axis=3)
    b_mid = b[:, :, :, C // 2 - 1:C // 2, :]
    b_last = b[:, :, :, C - 1:C, :]
    q_dec = qc * jnp.exp(b - b_mid)
    k_dec = kc * jnp.exp(b_mid - b)
    causal = jnp.tril(jnp.ones((C, C), dtype=bool))
    scores = jnp.where(causal, jnp.einsum('bhntk,bhnsk->bhnts', q_dec, k_dec), 0.0)
    o_intra = jnp.einsum('bhnts,bhnsv->bhntv', scores, vc)
    d_state = jnp.einsum('bhnsk,bhnsv->bhnkv', kc * jnp.exp(b_last - b), vc)
    chunk_decay = jnp.exp(b_last[:, :, :, 0, :])

    def step(state, inp):
        ds, dec = inp
        return dec[..., None] * state + ds, state

    s0 = jnp.zeros((bsz, H, K, K), f32)
    _, s_prev = lax.scan(step, s0, (jnp.moveaxis(d_state, 2, 0), jnp.moveaxis(chunk_decay, 2, 0)))
    s_prev = jnp.moveaxis(s_prev, 0, 2)
    o_inter = jnp.einsum('bhntk,bhnkv->bhntv', qc * jnp.exp(b), s_prev)
    o = (o_intra + o_inter).transpose(0, 2, 3, 1, 4).reshape(bsz, seq, H, K)
    o = head_rms(o, o_gain.reshape(H, K)).reshape(bsz, seq, A_WIDTH)
    o = o * jax.nn.silu(g.astype(f32))
    return o.astype(q.dtype)


def swa_sink_attention(q, k, v, q_gain, k_gain, sinks):
    f32 = jnp.float32
    bsz, seq, _ = q.shape
    nb = seq // BLOCK
    qh = head_rms(q.astype(f32).reshape(bsz, seq, B_KV_HEADS, B_GROUP, B_HEAD_DIM), q_gain)
    kh = head_rms(k.astype(f32).reshape(bsz, seq, B_KV_HEADS, B_HEAD_DIM), k_gain)
    vh = v.astype(f32).reshape(bsz, seq, B_KV_HEADS, B_HEAD_DIM)
    qb = qh.reshape(bsz, nb, BLOCK, B_KV_HEADS, B_GROUP, B_HEAD_DIM)
    kb = kh.reshape(bsz, nb, BLOCK, B_KV_HEADS, B_HEAD_DIM)
    vb = vh.reshape(bsz, nb, BLOCK, B_KV_HEADS, B_HEAD_DIM)

    def with_prev(t):
        prev = jnp.concatenate([jnp.zeros_like(t[:, :1]), t[:, :-1]], axis=1)
        return jnp.concatenate([prev, t], axis=2)

    kw, vw = with_prev(kb), with_prev(vb)
    scale = B_HEAD_DIM ** -0.5
    scores = jnp.einsum('bnqhgd,bnkhd->bnhgqk', qb, kw) * scale
    qi = jnp.arange(BLOCK)[:, None] + BLOCK
    ki = jnp.arange(2 * BLOCK)[None, :]
    rel = qi - ki
    band = (rel >= 0) & (rel < WINDOW)
    has_key = (jnp.arange(nb) > 0)[:, None, None] | (ki >= BLOCK)[None]
    mask = band[None] & has_key
    scores = jnp.where(mask[None, :, None, None], scores, -jnp.inf)
    sink = jnp.broadcast_to(sinks.astype(f32).reshape(B_KV_HEADS, B_GROUP)[None, None, :, :, None, None],
                            scores.shape[:-1] + (1,))
    probs = jax.nn.softmax(jnp.concatenate([scores, sink], axis=-1), axis=-1)[..., :-1]
    out = jnp.einsum('bnhgqk,bnkhd->bnqhgd', probs, vw)
    return out.reshape(bsz, seq, B_WIDTH).astype(q.dtype)


def setup_inputs(seed: int = 0) -> dict:
    key = jax.random.key(seed)
    ks = jax.random.split(key, 20)
    f32 = jnp.float32

    def w(k, shape, fan_in):
        return jax.random.normal(k, shape, f32) * (fan_in ** -0.5)

    return {
        "x": jax.random.normal(ks[0], (BATCH, SEQ, D_MODEL), f32),
        "c": jax.random.normal(ks[1], (BATCH, D_MODEL), f32),
        "w_ada": w(ks[2], (DEPTH, D_MODEL, N_MOD * D_MODEL), D_MODEL),
        "b_ada": 0.02 * jax.random.normal(ks[3], (DEPTH, N_MOD * D_MODEL), f32),
        "norm1_gain": 1.0 + 0.02 * jax.random.normal(ks[4], (DEPTH, D_MODEL), f32),
        "w_in": w(ks[5], (DEPTH, D_MODEL, IN_WIDTH), D_MODEL),
        "lb_logits": 0.5 * jax.random.normal(ks[6], (DEPTH + 1, A_WIDTH), f32),
        "hgrn_o_gain": 1.0 + 0.02 * jax.random.normal(ks[7], (DEPTH, A_WIDTH), f32),
        "q_norm_gain": 1.0 + 0.02 * jax.random.normal(ks[8], (DEPTH, B_HEAD_DIM), f32),
        "k_norm_gain": 1.0 + 0.02 * jax.random.normal(ks[9], (DEPTH, B_HEAD_DIM), f32),
        "sinks": 0.5 * jax.random.normal(ks[10], (DEPTH, B_Q_HEADS), f32),
        "w_branch_a": w(ks[11], (DEPTH, A_WIDTH, D_MODEL), A_WIDTH),
        "w_branch_b": w(ks[12], (DEPTH, B_WIDTH, D_MODEL), B_WIDTH),
        "w_out": w(ks[13], (DEPTH, D_MODEL, D_MODEL), D_MODEL),
        "norm2_gain": 1.0 + 0.02 * jax.random.normal(ks[14], (DEPTH, D_MODEL), f32),
        "w_mlp_in": w(ks[15], (DEPTH, D_MODEL, MLP_HIDDEN), D_MODEL),
        "w_mlp_out": w(ks[16], (DEPTH, MLP_HIDDEN, D_MODEL), MLP_HIDDEN),
    }


def reference(x, c, w_ada, b_ada, norm1_gain, w_in, lb_logits, hgrn_o_gain, q_norm_gain,
              k_norm_gain, sinks, w_branch_a, w_branch_b, w_out, norm2_gain, w_mlp_in, w_mlp_out):
    lb_all = jnp.cumsum(jax.nn.softmax(lb_logits.astype(jnp.float32), axis=0), axis=0)
    for l in range(DEPTH):
        mod = jax.nn.silu(c) @ w_ada[l] + b_ada[l]
        sh1, sc1, gt1, sh2, sc2, gt2 = (m[:, None, :] for m in jnp.split(mod, N_MOD, axis=-1))
        h = rms_norm(x, norm1_gain[l]) * (1.0 + sc1) + sh1
        qa, fa, ia, ga, qb, kb, vb, gate_a, gate_b = split_columns(h @ w_in[l])
        ya = hgrn2_mixer(qa, fa, ia, ga, lb_all[l], hgrn_o_gain[l]) @ w_branch_a[l]
        yb = swa_sink_attention(qb, kb, vb, q_norm_gain[l], k_norm_gain[l], sinks[l]) @ w_branch_b[l]
        merged = jax.nn.sigmoid(gate_a) * ya + jax.nn.sigmoid(gate_b) * yb
        x = x + gt1 * (merged @ w_out[l])
        h2 = rms_norm(x, norm2_gain[l]) * (1.0 + sc2) + sh2
        x = x + gt2 * (jnp.square(jax.nn.relu(h2 @ w_mlp_in[l])) @ w_mlp_out[l])
    return x
```

```python
import numpy as np
import ml_dtypes
from contextlib import ExitStack
import concourse.bass as bass
import concourse.mybir as mybir
from concourse.bass_utils import run_bass_kernel_spmd

F32 = mybir.dt.float32
BF16 = mybir.dt.bfloat16
AF = mybir.ActivationFunctionType
ALU = mybir.AluOpType
AX = mybir.AxisListType

ENGS = ("pe", "act", "dve", "pool", "sp")
EPS = 1e-6


class Buf:
    __slots__ = ("name", "last_w", "readers", "dma_sem", "dma_cnt", "psum")

    def __init__(self, name, psum=False):
        self.name = name
        self.psum = psum
        self.last_w = None
        self.readers = []
        self.dma_sem = None
        self.dma_cnt = 0


class Op:
    __slots__ = ("eng", "fn", "idx", "deps", "signal", "sigval", "is_dma",
                 "dma_buf", "dma_val", "waits")


class Sched:
    def __init__(self, nc):
        self.nc = nc
        self.ops = {e: [] for e in ENGS}
        self.dma_bufs = []

    def buf(self, name="b", psum=False):
        return Buf(name, psum)

    def op(self, eng, fn, reads=(), writes=(), dma=None):
        o = Op()
        o.eng, o.fn = eng, fn
        o.idx = len(self.ops[eng])
        o.signal = False
        o.sigval = None
        o.is_dma = dma is not None
        o.dma_buf = dma
        o.dma_val = 0
        o.waits = None
        if dma is not None:
            if dma.dma_sem is None:
                dma.dma_sem = True
                self.dma_bufs.append(dma)
            dma.dma_cnt += 16
            o.dma_val = dma.dma_cnt
        pr = [b for b in reads if b.psum]
        if pr:
            reads = [b for b in reads if not b.psum]
            writes = list(writes) + [b for b in pr if b not in writes]
        deps = []
        for b in reads:
            if b.last_w is not None:
                deps.append(b.last_w)
        for b in writes:
            w = b.last_w
            if w is not None:
                if w.eng == eng and not w.is_dma:
                    pass
                elif not (o.is_dma and w.is_dma and w.dma_buf is o.dma_buf):
                    deps.append(w)
            deps.extend(r for r in b.readers if r.eng != eng or r.is_dma)
        for b in reads:
            b.readers.append(o)
        for b in writes:
            b.last_w = o
            b.readers = []
        red = {}
        for d in deps:
            if d is o:
                continue
            if d.is_dma:
                k = ("d", id(d.dma_buf))
                if k not in red or red[k].dma_val < d.dma_val:
                    red[k] = d
            else:
                if d.eng == "pe" and eng == "pe":
                    continue
                k = ("e", d.eng)
                if k not in red or red[k].idx < d.idx:
                    red[k] = d
        o.deps = list(red.values())
        for d in o.deps:
            if d.is_dma and d.dma_val < d.dma_buf.dma_cnt:
                print("SCHED WARNING: partial DMA-semaphore wait on", d.dma_buf.name, d.dma_val, d.dma_buf.dma_cnt)
        self.ops[eng].append(o)
        return o

    def emit(self, final_wait_bufs=()):
        nc = self.nc
        for e in ENGS:
            for o in self.ops[e]:
                for d in o.deps:
                    if not d.is_dma:
                        d.signal = True
        for e in ENGS:
            c = 0
            for o in self.ops[e]:
                if o.signal and not o.is_dma:
                    c += 1
                    o.sigval = c
        with ExitStack() as st:
            esem = {e: st.enter_context(nc.semaphore("s_" + e)) for e in ENGS}
            for i, b in enumerate(self.dma_bufs):
                b.dma_sem = st.enter_context(nc.semaphore("d%d" % i))
            for e in ENGS:
                known = {}
                for o in self.ops[e]:
                    need = {}
                    for d in o.deps:
                        if d.is_dma:
                            key = ("d", id(d.dma_buf))
                            sem, val = d.dma_buf.dma_sem, d.dma_val
                        else:
                            if d.sigval is None:
                                continue
                            key = ("e", d.eng)
                            sem, val = esem[d.eng], d.sigval
                        if known.get(key, 0) >= val:
                            continue
                        if key not in need or need[key][1] < val:
                            need[key] = (sem, val)
                    for k, (sem, val) in need.items():
                        known[k] = val
                    o.waits = list(need.values())
            block = st.enter_context(nc.Block())

            def run(engname, eng):
                for o in self.ops[engname]:
                    for sem, val in o.waits:
                        eng.wait_ge(sem, val)
                    ins = o.fn(eng)
                    if o.is_dma:
                        ins.then_inc(o.dma_buf.dma_sem, 16)
                    elif o.signal:
                        ins.then_inc(esem[engname], 1)
                if engname == "sp":
                    for b in final_wait_bufs:
                        eng.wait_ge(b.dma_sem, b.dma_cnt)

            @block.tensor
            def _(pe):
                run("pe", pe)

            @block.scalar
            def _(act):
                run("act", act)

            @block.vector
            def _(dve):
                run("dve", dve)

            @block.gpsimd
            def _(pool):
                run("pool", pool)

            @block.sync
            def _(sp):
                run("sp", sp)


D = 2048
TCORE = 2048
TT = 1024
NT = TT // 128
KC = 16
NH = 8
C_QA, C_FA, C_IA, C_GA = 0, 1024, 2048, 3072
C_QB, C_KB, C_VB = 4096, 5120, 5376
C_GATEA, C_GATEB = 5632, 7680
NCF = 132 + 128 * 4 + 2

CFG = {"ng": 2, "prefix": True, "debug": False}


def _consts():
    s = np.arange(128)[:, None]
    t = np.arange(128)[None, :]
    same = (s // 64) == (t // 64)
    mid = (t // 64) * 64 + 31
    MA = np.zeros((128, 132), np.float32)
    MA[:, :128] = same * ((s <= t).astype(np.float32) - (s <= mid).astype(np.float32))
    for c in range(2):
        MA[:, 128 + c] = ((s[:, 0] // 64) == c) * (s[:, 0] <= c * 64 + 31)
        MA[:, 130 + c] = ((s[:, 0] // 64) == c)
    NM1 = -MA[:, :128]
    IDF = np.eye(128, dtype=np.float32)
    M2 = (same * (s > t)).astype(np.float32)
    MASKH = (same * (s <= t)).astype(np.float32)
    RM = np.stack([(np.arange(128) < 64), (np.arange(128) >= 64)], axis=1).astype(np.float32)
    cf = np.concatenate([MA, NM1, IDF, M2, MASKH, RM], axis=1).astype(np.float32)
    MC = (s <= t).astype(np.float32)
    MP = (s > t).astype(np.float32)
    cb = np.concatenate([IDF, MC, MP], axis=1).astype(ml_dtypes.bfloat16)
    return cf, cb, MP.astype(ml_dtypes.bfloat16)


def build_program():
    nc = bass.Bass("TRN2", target_bir_lowering=False)
    NG = CFG["ng"]
    DBG = CFG["debug"]

    def din(name, shape, dt=F32):
        return nc.dram_tensor(name, shape, dt, kind="ExternalInput").ap()

    x_own = din("x_own", [TCORE, D])
    x_pre = din("x_pre", [TCORE, D])
    flag_d = din("flag", [128, 1])
    c_col = din("c_col", [128, 16])
    w_ada = din("w_ada", [D, 6 * D])
    b_ada = din("b_ada", [1, 6 * D])
    n1c_d = din("n1c", [128, 16])
    n2c_d = din("n2c", [128, 16])
    w_in = din("w_in", [D, 9728])
    lbl = din("lb_logits", [2, 1024])
    ogain_d = din("hgrn_o_gain", [1, 1024])
    qg_d = din("q_norm_gain", [1, 64])
    kg_d = din("k_norm_gain", [1, 64])
    sinks_d = din("sinks", [1, 16])
    w_a = din("w_branch_a", [1024, D])
    w_b = din("w_branch_b", [1024, D])
    w_out = din("w_out", [D, D])
    w1 = din("w_mlp_in", [D, 4 * D])
    w2 = din("w_mlp_out", [4 * D, D])
    cf_d = din("cf", [128, NCF])
    cb_d = din("cb", [128, 384], BF16)
    mp0_d = din("mp0", [128, 128], BF16)
    out_d = nc.dram_tensor("out", [TCORE, D], F32, kind="ExternalOutput").ap()
    mod_d = nc.dram_tensor("mod_d", [1, 6 * D], F32).ap()
    lnoml_d = nc.dram_tensor("lnoml_d", [1, 1024], F32).ap()
    dbg = {}
    if DBG:
        dbg["hT"] = nc.dram_tensor("dbg_hT", [128, 16, TT], BF16, kind="ExternalOutput").ap()
        dbg["oaT"] = nc.dram_tensor("dbg_oaT", [128, 8, TT], BF16, kind="ExternalOutput").ap()
        dbg["obT"] = nc.dram_tensor("dbg_obT", [128, 8, TT], BF16, kind="ExternalOutput").ap()
        dbg["mT"] = nc.dram_tensor("dbg_mT", [128, 16, TT], BF16, kind="ExternalOutput").ap()
        dbg["x1"] = nc.dram_tensor("dbg_x1", [128, 8, D], F32, kind="ExternalOutput").ap()
        dbg["mod"] = nc.dram_tensor("dbg_mod", [1, 6 * D], F32, kind="ExternalOutput").ap()
        dbg["st"] = nc.dram_tensor("dbg_st", [128, 8, 128], F32, kind="ExternalOutput").ap()

    S = Sched(nc)
    st = ExitStack()
    with st:
        def sb(name, shape, dt):
            return st.enter_context(nc.sbuf_tensor(name, shape, dt))

        X = sb("X", [128, 16384], F32)
        M = sb("M", [128, 8192], F32)
        HT = sb("HT", [128, 16, TT], BF16)
        W = [sb("W%d" % i, [128, 8192], BF16) for i in range(3)]
        CF = sb("CF", [128, NCF], F32)
        CB = sb("CB", [128, 384], BF16)
        MP0 = sb("MP0", [128, 128], BF16)
        ST = sb("ST", [128, 8, 128], F32)
        P2 = sb("P2", [128, 4096], F32)
        KH = sb("KH", [128, 4, 128], BF16)
        VH = sb("VH", [128, 4, 65], BF16)
        SM = sb("SM", [128, 256], F32)
        cT = sb("cT", [128, 16], BF16)
        PS = [st.enter_context(nc.psum_tensor("ps%d" % i, [128, 512], F32)) for i in range(8)]
        PSB = [p[:].bitcast(BF16) for p in PS]

        MA = CF[:, 0:132]
        NM1 = CF[:, 132:260]
        IDF = CF[:, 260:388]
        M2c = CF[:, 388:516]
        MASKH = CF[:, 516:644]
        RMc = CF[:, 644:646]
        IDB = CB[:, 0:128]
        MCm = CB[:, 128:256]
        MPm = CB[:, 256:384]

        g1c = SM[:, 0:16]
        sh1c = SM[:, 16:32]
        g2c = SM[:, 32:48]
        sh2c = SM[:, 48:64]
        esink = SM[:, 64:80]
        flagt = SM[:, 80:81]
        ccol = SM[:, 96:112]
        ctmp = SM[:, 112:128]
        tmpc = SM[:, 128:144]
        qgb = SM[:, 144:208].unsqueeze(1)
        kgb = P2[:, 4032:4096]
        stat = sb("stat", [128, 64], F32)

        LNOML = P2[:, 0:1024]
        OG = P2[:, 1024:2048]
        GT2 = P2[:, 0:2048]
        STG = P2[:, 2048:4032]

        b_const = S.buf("const")
        b_small = S.buf("small")
        b_modd = S.buf("modd")
        b_lnd = S.buf("lnd")
        wb = [S.buf("w%d" % i) for i in range(3)]
        bq = [S.buf("ps%d" % k, psum=True) for k in range(8)]
        b_h = [[S.buf("h%d_%d" % (kc, t)) for t in range(NT)] for kc in range(KC)]
        b_ST = [S.buf("st%d" % j) for j in range(NH)]
        b_halo = S.buf("halo")
        b_p2a = S.buf("p2a")
        b_p2b = S.buf("p2b")
        b_stg = [S.buf("stg%d" % i) for i in range(4)]
        b_stat = [S.buf("stat%d" % i) for i in range(16)]
        b_out = S.buf("out")
        b_dbg = S.buf("dbg")
        b_oaT = [[S.buf("oa") for t in range(NT)] for j in range(8)]
        b_obT = [S.buf("ob%d" % n) for n in range(NT)]
        b_xres = [[S.buf("xr") for pc in range(4)] for t in range(NT)]
        b_ta = [S.buf("ta%d" % i) for i in range(48)]
        b_m = [S.buf("m%d" % i) for i in range(48)]
        b_mT = [[S.buf("mT") for hf in range(2)] for c in range(16)]
        b_hid = [[S.buf("hid") for hf in range(2)] for c in range(16)]
        b_hte = [S.buf("hte%d" % i) for i in range(4)]
        b_fence = S.buf("fence")

        bank_ctr = [0]

        bank_pinned = set()

        def nb(pin=False):
            assert len(bank_pinned) < 8, "all PSUM banks pinned"
            while True:
                k = bank_ctr[0] % 8
                bank_ctr[0] += 1
                if k not in bank_pinned:
                    if pin:
                        bank_pinned.add(k)
                    return k

        def unpin(k):
            bank_pinned.discard(k)

        def bqs(k, lo=0, hi=512):
            return [bq[k]]

        def flat(ll):
            r = []
            for l in ll:
                if isinstance(l, (list, tuple)):
                    r.extend(flat(l))
                else:
                    r.append(l)
            return r

        def fence(bufs):
            bl = flat(bufs)
            S.op("dve", lambda e: e.memset(stat[:, 63:64], 0.0), writes=bl + [b_fence])

        wcnt = [0]
        wpinned = set()

        def wslot():
            while True:
                i = wcnt[0] % 3
                wcnt[0] += 1
                if i not in wpinned:
                    return i

        def wpin(buf):
            wpinned.add(wb.index(buf))

        def wunpin(buf):
            wpinned.discard(wb.index(buf))

        def wload(srcs, ncols, nk=16):
            i = wslot()
            view = W[i][:, 0:nk * ncols].rearrange("p (a b) -> p a b", b=ncols)
            for (src, off) in srcs:
                n = src.shape[1]
                S.op("pool", lambda e, src=src, off=off, n=n: e.dma_start(
                    out=view[:, :, off:off + n], in_=src.rearrange("(kc p) n -> p kc n", p=128)),
                    writes=[wb[i]], dma=wb[i])
            return view, wb[i]

        def mm_group(k, col0, pairs, reads, outrows=None, ncol=None):
            n = ncol if ncol is not None else pairs[0][1].shape[-1]
            oap = PS[k][:, col0:col0 + n] if outrows is None else PS[k][outrows[0]:outrows[1], col0:col0 + n]

            def fn(pe):
                ins = None
                L = len(pairs)
                for idx, (l, r) in enumerate(pairs):
                    ins = pe.matmul(oap, lhsT=l, rhs=r, start=(idx == 0), stop=(idx == L - 1))
                return ins
            S.op("pe", fn, reads=reads, writes=bqs(k, col0, col0 + n))

        def silu_chain(k, c0, n, stg_ap, stg_buf):
            src = PS[k][:, c0:c0 + n]
            rb = bqs(k, c0, c0 + n)
            S.op("act", lambda e: e.activation(out=stg_ap, in_=src, func=AF.Exp, scale=-1.0), reads=rb, writes=[stg_buf])
            S.op("act", lambda e: e.activation(out=stg_ap, in_=stg_ap, func=AF.Ln, bias=1.0), reads=[stg_buf], writes=[stg_buf])
            S.op("act", lambda e: e.activation(out=stg_ap, in_=stg_ap, func=AF.Exp, scale=-1.0), reads=[stg_buf], writes=[stg_buf])

        def rstd_ops(ss_ap, out_ap, n_inv, bss, bout):
            S.op("act", lambda e: e.activation(out=out_ap, in_=ss_ap, func=AF.Ln, scale=n_inv, bias=EPS), reads=[bss], writes=[bout])
            S.op("act", lambda e: e.activation(out=out_ap, in_=out_ap, func=AF.Exp, scale=-0.5), reads=[bout], writes=[bout])

        def ld(dst, src, buf, **kw):
            S.op("sp", lambda e: e.dma_start(out=dst, in_=src, **kw), writes=[buf], dma=buf)

        ld(CF[:], cf_d, b_const)
        ld(CB[:], cb_d, b_const)
        ld(MP0[:], mp0_d, b_const)
        ld(flagt, flag_d, b_const)
        ld(ccol, c_col, b_const)
        ld(tmpc, n1c_d, b_const)
        ld(SM[:, 224:240], n2c_d, b_const)
        n2c = SM[:, 224:240]
        ld(SM[:, 144:208], qg_d[0, :].partition_broadcast(128), b_const)
        ld(kgb, kg_d[0, :].partition_broadcast(128), b_const)
        ld(esink, sinks_d[0, :].partition_broadcast(128), b_const)
        S.op("dve", lambda e: e.memset(ST[:].rearrange("p a b -> p (a b)"), 0.0), writes=b_ST)
        S.op("dve", lambda e: e.memset(KH[:].rearrange("p a b -> p (a b)"), 0.0), writes=[b_halo])
        S.op("dve", lambda e: e.memset(VH[:].rearrange("p a b -> p (a b)"), 0.0), writes=[b_halo])
        S.op("dve", lambda e: e.memset(VH[:, :, 64:65], 1.0), writes=[b_halo])
        S.op("dve", lambda e: e.tensor_scalar(out=SM[:, 144:208], in0=SM[:, 144:208], scalar1=0.125, scalar2=None, op0=ALU.mult), reads=[b_const], writes=[b_small])
        S.op("act", lambda e: e.activation(out=esink, in_=esink, func=AF.Exp), reads=[b_const], writes=[b_small])
        S.op("act", lambda e: e.activation(out=ctmp, in_=ccol, func=AF.Exp, scale=-1.0), reads=[b_const], writes=[b_stat[0]])
        S.op("act", lambda e: e.activation(out=ctmp, in_=ctmp, func=AF.Ln, bias=1.0), reads=[b_stat[0]], writes=[b_stat[0]])
        S.op("act", lambda e: e.activation(out=ctmp, in_=ctmp, func=AF.Exp, scale=-1.0), reads=[b_stat[0]], writes=[b_stat[0]])
        S.op("dve", lambda e: e.tensor_tensor(out=cT[:], in0=ccol, in1=ctmp, op=ALU.mult), reads=[b_stat[0], b_const], writes=[b_small])
        adarow = sb("adarow", [1, 1024], F32)
        mrow = [STG[0:1, 0:512], STG[0:1, 512:1024]]
        brow = [adarow[0:1, 0:512], adarow[0:1, 512:1024]]
        b_brow = [S.buf("brow0"), S.buf("brow1")]
        b_moddp = [S.buf("modd_a"), S.buf("modd_b")]
        b_modd2p = [S.buf("modd2_a"), S.buf("modd2_b")]

        def ada_piece(pc):
            i = pc % 2
            mb = b_moddp[i] if pc < 8 else b_modd2p[i]
            S.op("sp", lambda e: e.dma_start(out=brow[i], in_=b_ada[:, pc * 512:(pc + 1) * 512]), writes=[b_brow[i]], dma=b_brow[i])
            wv, wbuf = wload([(w_ada[:, pc * 512:(pc + 1) * 512], 0)], 512)
            k = nb()
            mm_group(k, 0, [(cT[:, kc:kc + 1], wv[:, kc, :]) for kc in range(KC)], [wbuf, b_small], outrows=(0, 1))
            S.op("dve", lambda e: e.tensor_tensor(out=mrow[i], in0=PS[k][0:1, 0:512], in1=brow[i], op=ALU.add),
                 reads=bqs(k) + [b_brow[i]], writes=[b_stg[i]])
            S.op("sp", lambda e: e.dma_start(out=mod_d[:, pc * 512:(pc + 1) * 512], in_=mrow[i]), reads=[b_stg[i]], writes=[mb], dma=mb)

        def ada_extra(pg, j):
            def f():
                ada_piece(8 + 2 * j)
                ada_piece(9 + 2 * j)
            return f

        n_up = 8 if CFG["prefix"] else 24
        for pc in range(n_up):
            ada_piece(pc)

        def ldcol(dst, off, buf, src_buf):
            S.op("sp", lambda e: e.dma_start(out=dst, in_=mod_d[0, off:off + D].rearrange("(kc p) -> p kc", p=128), allow_slow_non_contiguous=True),
                 reads=list(src_buf), writes=[buf], dma=buf)

        b_mc = S.buf("modcols")
        b_mc2 = S.buf("modcols2")
        ldcol(sh1c, 0, b_mc, b_moddp)
        ldcol(g1c, D, b_mc, b_moddp)
        S.op("dve", lambda e: e.scalar_tensor_tensor(out=g1c, in0=g1c, scalar=1.0, in1=tmpc, op0=ALU.add, op1=ALU.mult), reads=[b_mc, b_const], writes=[b_small])

        def load_mod2():
            ldcol(sh2c, 3 * D, b_mc2, b_modd2p)
            ldcol(g2c, 4 * D, b_mc2, b_modd2p)
            S.op("dve", lambda e: e.scalar_tensor_tensor(out=g2c, in0=g2c, scalar=1.0, in1=n2c, op0=ALU.add, op1=ALU.mult), reads=[b_mc2, b_const], writes=[b_mc2])
        if not CFG["prefix"]:
            load_mod2()
        if DBG:
            pass
        lbt = M[:, 0:2048].rearrange("p (a b) -> p a b", b=1024)
        ld(lbt, lbl.partition_broadcast(128), b_m[0])
        S.op("dve", lambda e: e.tensor_tensor(out=lbt[:, 0, :], in0=lbt[:, 1, :], in1=lbt[:, 0, :], op=ALU.subtract), reads=[b_m[0]], writes=[b_m[0]])
        S.op("act", lambda e: e.activation(out=lbt[:, 0, :], in_=lbt[:, 0, :], func=AF.Exp, scale=-1.0), reads=[b_m[0]], writes=[b_m[0]])
        S.op("act", lambda e: e.activation(out=lbt[:, 0, :], in_=lbt[:, 0, :], func=AF.Ln, bias=1.0), reads=[b_m[0]], writes=[b_m[0]])
        S.op("dve", lambda e: e.tensor_scalar(out=lbt[0:1, 1, :], in0=lbt[0:1, 0, :], scalar1=-1.0, scalar2=None, op0=ALU.mult), reads=[b_m[0]], writes=[b_m[0]])
        S.op("sp", lambda e: e.dma_start(out=lnoml_d, in_=lbt[0:1, 1, :]), reads=[b_m[0]], writes=[b_lnd], dma=b_lnd)

        TA0 = 8192

        def TAf(off, n):
            return X[:, TA0 + off:TA0 + off + n]

        def TAb(off, n):
            return X[:, TA0 + off:TA0 + off + n].bitcast(BF16)

        oaT = X[:, 0:4096].bitcast(BF16).rearrange("p (c t) -> p c t", t=TT)
        obT = X[:, 4096:8192].bitcast(BF16).rearrange("p (c t) -> p c t", t=TT)
        xres = X[:].rearrange("p (t d) -> p t d", d=D)
        mergedT = M[:].bitcast(BF16).rearrange("p (c t) -> p c t", t=TT)
        hidT = mergedT

        def pipeline(items):
            n = len(items)
            ns = max(len(x) for x in items)
            for step in range(n + ns - 1):
                for st_ in reversed(range(ns)):
                    it = step - st_
                    if 0 <= it < n and st_ < len(items[it]):
                        items[it][st_]()

        def norm_phase(src_dram, tok0, gc, shc, xt_views, junk_view, tb, xres_src=False):
            def front(t):
                i = t % 2
                xt = xt_views[i]
                bxt = tb[i]
                bjk = tb[2]
                ss = stat[:, i:i + 1]
                rs = stat[:, 2 + i:3 + i]
                if not xres_src:
                    S.op("sp", lambda e, xt=xt, t=t: e.dma_start(out=xt, in_=src_dram[tok0 + t * 128:tok0 + (t + 1) * 128, :]), writes=[bxt], dma=bxt)
                    srcap, srcb = xt, [bxt]
                else:
                    srcap, srcb = xres[:, t, :], b_xres[t]
                S.op("act", lambda e, srcap=srcap, ss=ss: e.activation(out=junk_view, in_=srcap, func=AF.Square, accum_out=ss), reads=srcb, writes=[bjk, b_stat[i]])
                rstd_ops(ss, rs, 1.0 / D, b_stat[i], b_stat[2 + i])
                S.op("dve", lambda e, xt=xt, srcap=srcap, rs=rs: e.tensor_scalar(out=xt, in0=srcap, scalar1=rs, scalar2=None, op0=ALU.mult),
                     reads=srcb + [b_stat[2 + i]], writes=[bxt])

            def back(t):
                i = t % 2
                xt = xt_views[i]
                bxt = tb[i]
                for q4 in range(4):
                    k = nb()

                    def trs(pe, k=k, q4=q4, xt=xt):
                        ins = None
                        for j in range(4):
                            kc = q4 * 4 + j
                            ins = pe.transpose(PS[k][:, j * 128:(j + 1) * 128], xt[:, kc * 128:(kc + 1) * 128], IDF)
                        return ins
                    S.op("pe", trs, reads=[bxt, b_const], writes=bqs(k))
                    eng = "act" if q4 % 2 == 0 else "dve"

                    def ev(e, k=k, q4=q4, t=t, eng=eng):
                        ins = None
                        for j in range(4):
                            kc = q4 * 4 + j
                            o = HT[:, kc, t * 128:(t + 1) * 128]
                            src = PS[k][:, j * 128:(j + 1) * 128]
                            if eng == "act":
                                ins = e.activation(out=o, in_=src, func=AF.Identity, scale=gc[:, kc:kc + 1], bias=shc[:, kc:kc + 1])
                            else:
                                ins = e.tensor_scalar(out=o, in0=src, scalar1=gc[:, kc:kc + 1], scalar2=shc[:, kc:kc + 1], op0=ALU.mult, op1=ALU.add)
                        return ins
                    S.op(eng, ev, reads=bqs(k) + [b_small, b_mc, b_mc2], writes=[b_h[q4 * 4 + j][t] for j in range(4)])
            pipeline([[lambda t=t: front(t), lambda t=t: back(t)] for t in range(NT)])

        def hreads(tiles):
            return [b_h[kc][t] for kc in range(KC) for t in tiles]

        def hg_temps(slot):
            if slot == 0:
                base, off0, bl = X, TA0, b_ta
            else:
                base, off0, bl = M, 0, b_m

            def Ff(off, n):
                return base[:, off0 + off:off0 + off + n]

            def Fb(off, n):
                return base[:, off0 + off:off0 + off + n].bitcast(BF16)
            T = {}
            T["qT"] = Ff(0, 1024)
            T["kk_all"], T["lf_all"], T["lk_all"] = Ff(1024, 1024), Ff(2048, 1024), Ff(3072, 1024)
            T["kk"] = Ff(1024, 1024).rearrange("p (t f) -> p t f", f=128)
            T["lf"] = Ff(2048, 1024).rearrange("p (t f) -> p t f", f=128)
            T["lk"] = Ff(3072, 1024).rearrange("p (t f) -> p t f", f=128)
            T["vv"] = Fb(4096, 512).rearrange("p (t f) -> p t f", f=128)
            T["GG"] = Fb(4608, 512).rearrange("p (t f) -> p t f", f=128)
            T["stq"] = [Ff(5120, 512), Ff(5632, 512)]
            T["stg"] = [Ff(6144, 128), Ff(6272, 128)]
            T["stg2"] = [Ff(6400, 128), Ff(6528, 128)]
            T["AT"] = [Ff(6656, 128), Ff(6784, 128)]
            T["E3"] = [Ff(6912, 128), Ff(7040, 128)]
            T["qd"] = [Fb(7168, 64), Fb(7232, 64)]
            T["kd"] = [Fb(7296, 64), Fb(7360, 64)]
            T["kd2"] = [[Fb(7424, 64), Fb(6400, 64)], [Fb(7488, 64), Fb(6464, 64)]]
            T["scm"] = [Fb(7552, 64), Fb(7616, 64)]
            T["ofin"] = [Fb(7680, 64), Fb(7744, 64)]
            T["em"] = [Ff(7808, 4), Ff(7812, 4)]
            T["Ssc"] = [[Fb(7824, 64), Fb(7888, 64)], [Fb(7952, 64), Fb(8016, 64)]]
            T["osq"] = Fb(8080, 64)
            B = {}
            B["qT"], B["kk"], B["lf"], B["lk"], B["v"], B["G"] = (bl[i] for i in range(6))
            B["stq"] = [bl[6], bl[7]]
            B["stg"] = [bl[8], bl[9]]
            B["stg2"] = [bl[10], bl[11]]
            B["AT"] = [bl[12], bl[13]]
            B["E3"] = [bl[14], bl[15]]
            B["qd"] = [bl[16], bl[17]]
            B["kd"] = [bl[18], bl[19]]
            B["kd2"] = [bl[20], bl[21]]
            B["scm"] = [bl[22], bl[23]]
            B["ofin"] = [bl[24], bl[25]]
            B["em"] = [bl[26], bl[27]]
            B["Ssc"] = [[bl[28], bl[29]], [bl[30], bl[31]]]
            B["osq"] = bl[32]
            T["ss"] = [stat[:, 8 + 4 * slot + i:9 + 4 * slot + i] for i in range(2)]
            T["rs"] = [stat[:, 10 + 4 * slot + i:11 + 4 * slot + i] for i in range(2)]
            B["ss"] = [b_stat[4 + 4 * slot + i] for i in range(2)]
            B["rs"] = [b_stat[6 + 4 * slot + i] for i in range(2)]
            return T, B

        HG = [hg_temps(0), hg_temps(1)]

        def hg_temps_pf(slot):
            if slot < 2:
                base, off0, bl = X, TA0 + (slot % 2) * 4096, b_ta
            else:
                base, off0, bl = M, (slot % 2) * 4096, b_m
            nbufs = [S.buf("pf") for _ in range(12)]
            bl.extend(nbufs)

            def Ff(off, n):
                return base[:, off0 + off:off0 + off + n]

            def Fb(off, n):
                return base[:, off0 + off:off0 + off + n].bitcast(BF16)
            T = {n_: None for n_ in ("qT", "GG", "AT", "qd", "kd", "scm", "ofin", "Ssc")}
            T["kk_all"], T["lf_all"], T["lk_all"] = Ff(0, 1024), Ff(1024, 1024), Ff(2048, 1024)
            T["kk"] = Ff(0, 1024).rearrange("p (t f) -> p t f", f=128)
            T["lf"] = Ff(1024, 1024).rearrange("p (t f) -> p t f", f=128)
            T["lk"] = Ff(2048, 1024).rearrange("p (t f) -> p t f", f=128)
            T["vv"] = Fb(3072, 512).rearrange("p (t f) -> p t f", f=128)
            T["E3"] = [Ff(3584, 128), Ff(3712, 128)]
            T["kd2"] = [[Fb(3840, 64), Fb(3904, 64)], [Fb(3968, 64), Fb(4032, 64)]]
            T["em"] = [stat[:, 16 + slot * 8 + i * 4:20 + slot * 8 + i * 4] for i in range(2)]
            B = {"kk": nbufs[0], "lf": nbufs[1], "lk": nbufs[2], "v": nbufs[3], "E3": [nbufs[4], nbufs[5]],
                 "kd2": [nbufs[6], nbufs[7]], "em": [nbufs[8], nbufs[9]]}
            return T, B

        HGP = [hg_temps_pf(i) for i in range(4)]

        def hgrn_gen(j, own, slot, extra=None, wl=None):
            T, B = HG[slot] if own else HGP[slot]
            qT, kk, lf, lk, vv, GG = T["qT"], T["kk"], T["lf"], T["lk"], T["vv"], T["GG"]
            def load_head(jj):
                if own:
                    wv_, wb_ = wload([(w_in[:, C_QA + jj * 128:C_QA + (jj + 1) * 128], 0),
                                      (w_in[:, C_FA + jj * 128:C_FA + (jj + 1) * 128], 128),
                                      (w_in[:, C_IA + jj * 128:C_IA + (jj + 1) * 128], 256),
                                      (w_in[:, C_GA + jj * 128:C_GA + (jj + 1) * 128], 384)], 512)
                else:
                    wv_, wb_ = wload([(w_in[:, C_FA + jj * 128:C_FA + (jj + 1) * 128], 0),
                                      (w_in[:, C_IA + jj * 128:C_IA + (jj + 1) * 128], 128)], 256)
                wpin(wb_)
                return wv_, wb_
            if j not in wl:
                wl[j] = load_head(j)
            wv, wbuf = wl[j]
            if j + 1 < NH and (j + 1) not in wl:
                wl[j + 1] = load_head(j + 1)
            fo, nfi = (128, 384) if own else (0, 256)
            if own:
                for hf in range(2):
                    k = nb(pin=True)
                    mm_group(k, 0, [(wv[:, kc, 0:128], HT[:, kc, hf * 512:(hf + 1) * 512]) for kc in range(KC)],
                             [wbuf] + hreads(range(hf * 4, hf * 4 + 4)))
                    yield
                    silu_chain(k, 0, 512, T["stq"][hf], B["stq"][hf])
                    S.op("dve", lambda e, k=k, hf=hf: e.tensor_tensor(out=qT[:, hf * 512:(hf + 1) * 512], in0=PS[k][:, 0:512], in1=T["stq"][hf], op=ALU.mult),
                         reads=bqs(k) + [B["stq"][hf]], writes=[B["qT"]])
                    unpin(k)
                    yield
            for t in range(NT):
                i = t % 2
                k = nb(pin=True)
                mm_group(k, 0, [(HT[:, kc, t * 128:(t + 1) * 128], wv[:, kc, fo:fo + nfi]) for kc in range(KC)],
                         [wbuf] + hreads([t]))
                yield
                S.op("act", lambda e, k=k, t=t: e.activation(out=kk[:, t, :], in_=PS[k][:, 0:128], func=AF.Exp), reads=bqs(k), writes=[B["kk"]])
                S.op("dve", lambda e, k=k, t=t: e.tensor_copy(out=vv[:, t, :], in_=PS[k][:, 128:256]), reads=bqs(k), writes=[B["v"]])
                if own:
                    silu_chain(k, 256, 128, T["stg"][i], B["stg"][i])
                    S.op("dve", lambda e, k=k, i=i: e.tensor_tensor(out=T["stg"][i], in0=PS[k][:, 256:384], in1=T["stg"][i], op=ALU.mult),
                         reads=bqs(k) + [B["stg"][i]], writes=[B["stg"][i]])
                    S.op("dve", lambda e, t=t, i=i: e.tensor_tensor(out=GG[:, t, :], in0=T["stg"][i], in1=OG[:, j * 128:(j + 1) * 128], op=ALU.mult),
                         reads=[B["stg"][i], b_p2a], writes=[B["G"]])
                unpin(k)
                yield
            wunpin(wbuf)
            S.op("act", lambda e: e.activation(out=T["kk_all"], in_=T["kk_all"], func=AF.Ln, bias=1.0), reads=[B["kk"]], writes=[B["kk"]])
            for hh in range(2):
                S.op("dve", lambda e, hh=hh: e.tensor_tensor(out=lk[:, hh * 4:(hh + 1) * 4, :], in0=LNOML[:, j * 128:(j + 1) * 128].unsqueeze(1).broadcast_to([128, 4, 128]),
                                                        in1=kk[:, hh * 4:(hh + 1) * 4, :], op=ALU.subtract),
                     reads=[B["kk"], b_p2a], writes=[B["lk"]])
            S.op("act", lambda e: e.activation(out=T["kk_all"], in_=T["lk_all"], func=AF.Exp), reads=[B["lk"]], writes=[B["kk"]])
            S.op("act", lambda e: e.activation(out=T["lf_all"], in_=T["kk_all"], func=AF.Ln, scale=-1.0, bias=1.0), reads=[B["kk"]], writes=[B["lf"]])
            yield "spawn"
            if extra is not None:
                extra()
                yield
            STj = ST[:, j, :]
            AT, E3, qd, kd, kd2, scm, ofin, em, Ssc = (T[n_] for n_ in ("AT", "E3", "qd", "kd", "kd2", "scm", "ofin", "em", "Ssc"))

            def front(t):
                i = t % 2
                kx = nb(pin=True)
                ky = nb(pin=True) if own else kx
                dcol = (256, 384) if own else (0, 256)
                if own:
                    def cum(pe, kx=kx, t=t):
                        pe.matmul(PS[kx][:, 0:132], lhsT=lf[:, t, :], rhs=MA, start=True, stop=True)
                        pe.matmul(PS[kx][:, 256:384], lhsT=lk[:, t, :], rhs=IDF, start=True, stop=False)
                        pe.matmul(PS[kx][:, 256:384], lhsT=lf[:, t, :], rhs=NM1, start=False, stop=True)
                        return pe.matmul(PS[kx][:, 384:512], lhsT=M2c, rhs=lf[:, t, :], start=True, stop=True)
                    S.op("pe", cum, reads=[B["lf"], B["lk"], b_const], writes=bqs(kx))
                    yield
                    S.op("act", lambda e: e.activation(out=AT[i], in_=PS[kx][:, 0:128], func=AF.Exp), reads=bqs(kx), writes=[B["AT"][i]])
                    S.op("act", lambda e: e.activation(out=em[i], in_=PS[kx][:, 128:132], func=AF.Exp), reads=bqs(kx), writes=[B["em"][i]])
                    S.op("act", lambda e: e.activation(out=kd[i], in_=PS[kx][:, 256:384], func=AF.Exp), reads=bqs(kx), writes=[B["kd"][i]])
                else:
                    def cum(pe, kx=kx, t=t):
                        pe.matmul(PS[kx][:, 128:132], lhsT=lf[:, t, :], rhs=MA[:, 128:132], start=True, stop=True)
                        return pe.matmul(PS[kx][:, 384:512], lhsT=M2c, rhs=lf[:, t, :], start=True, stop=True)
                    S.op("pe", cum, reads=[B["lf"], b_const], writes=bqs(kx))
                    yield
                    S.op("act", lambda e: e.activation(out=em[i], in_=PS[kx][:, 128:132], func=AF.Exp), reads=bqs(kx), writes=[B["em"][i]])
                S.op("act", lambda e: e.activation(out=E3[i], in_=PS[kx][:, 384:512], func=AF.Exp), reads=bqs(kx), writes=[B["E3"][i]])
                yield
                for c in range(2):
                    S.op("dve", lambda e, c=c: e.scalar_tensor_tensor(out=kd2[i][c], in0=E3[i], scalar=RMc[:, c:c + 1], in1=kk[:, t, :], op0=ALU.mult, op1=ALU.mult),
                         reads=[B["E3"][i], B["kk"], b_const], writes=[B["kd2"][i]])
                if own:
                    S.op("dve", lambda e: e.tensor_tensor(out=qd[i], in0=qT[:, t * 128:(t + 1) * 128], in1=AT[i], op=ALU.mult), reads=[B["qT"], B["AT"][i]], writes=[B["qd"][i]])
                    yield
                    S.op("pe", lambda pe: pe.matmul(PS[ky][:, 0:128], lhsT=kd[i], rhs=qd[i], start=True, stop=True),
                         reads=[B["kd"][i], B["qd"][i]], writes=bqs(ky))
                    yield
                    S.op("dve", lambda e: e.tensor_tensor(out=scm[i], in0=PS[ky][:, 0:128], in1=MASKH, op=ALU.mult), reads=bqs(ky) + [b_const], writes=[B["scm"][i]])
                yield

                def dst(pe):
                    pe.matmul(PS[ky][:, dcol[0]:dcol[0] + 128], lhsT=kd2[i][0], rhs=vv[:, t, :], start=True, stop=True)
                    return pe.matmul(PS[ky][:, dcol[1]:dcol[1] + 128], lhsT=kd2[i][1], rhs=vv[:, t, :], start=True, stop=True)
                S.op("pe", dst, reads=[B["kd2"][i], B["v"]], writes=bqs(ky))
                if own:
                    unpin(kx)
                yield
                banks[t] = (kx, ky, dcol)

            def back(t):
                i = t % 2
                kx, ky, dcol = banks[t]
                dsrc = [(PS[ky][:, dcol[0]:dcol[0] + 128], bqs(ky)), (PS[ky][:, dcol[1]:dcol[1] + 128], bqs(ky))]
                for c in range(2):
                    if own:
                        S.op("act", lambda e, c=c: e.activation(out=Ssc[i][c], in_=STj, func=AF.Identity, scale=em[i][:, c:c + 1]),
                             reads=[b_ST[j], B["em"][i]], writes=[B["Ssc"][i][c]])
                    S.op("dve", lambda e, c=c: e.scalar_tensor_tensor(out=STj, in0=STj, scalar=em[i][:, 2 + c:3 + c], in1=dsrc[c][0], op0=ALU.mult, op1=ALU.add),
                         reads=[b_ST[j], B["em"][i]] + dsrc[c][1], writes=[b_ST[j]])
                    yield
                if own:
                    def omm(pe):
                        pe.matmul(PS[ky][:, 128:256], lhsT=scm[i], rhs=vv[:, t, :], start=True, stop=False)
                        pe.matmul(PS[ky][0:64, 128:256], lhsT=qd[i][:, 0:64], rhs=Ssc[i][0], start=False, stop=True)
                        return pe.matmul(PS[ky][64:128, 128:256], lhsT=qd[i][:, 64:128], rhs=Ssc[i][1], start=False, stop=True)
                    S.op("pe", omm, reads=[B["scm"][i], B["v"], B["qd"][i], B["Ssc"][i][0], B["Ssc"][i][1]], writes=bqs(ky))
                    yield
                    ss, rs = T["ss"][i], T["rs"][i]
                    S.op("act", lambda e: e.activation(out=T["osq"], in_=PS[ky][:, 128:256], func=AF.Square, accum_out=ss), reads=bqs(ky), writes=[B["osq"], B["ss"][i]])
                    rstd_ops(ss, rs, 1.0 / 128, B["ss"][i], B["rs"][i])
                    yield
                    S.op("dve", lambda e: e.scalar_tensor_tensor(out=ofin[i], in0=PS[ky][:, 128:256], scalar=rs, in1=GG[:, t, :], op0=ALU.mult, op1=ALU.mult),
                         reads=bqs(ky) + [B["rs"][i], B["G"]], writes=[B["ofin"][i]])
                    yield
                    unpin(ky)
                    kz = nb(pin=True)
                    S.op("pe", lambda pe: pe.transpose(PSB[kz][:, 0:128], ofin[i], IDB), reads=[B["ofin"][i], b_const], writes=bqs(kz))
                    yield
                    S.op("act", lambda e: e.activation(out=oaT[:, j, t * 128:(t + 1) * 128], in_=PSB[kz][:, 0:128], func=AF.Copy), reads=bqs(kz), writes=[b_oaT[j][t]])
                    unpin(kz)
                    yield
                else:
                    unpin(ky)

            banks = {}
            if own:
                yield from front(0)
                for t in range(NT):
                    if t + 1 < NT:
                        yield from front(t + 1)
                    yield from back(t)
            else:
                for t in range(NT):
                    yield from front(t)
                    yield from back(t)

        def run_interleaved(gens, max_active=2):
            pending = list(gens)
            active = []
            credit = 1
            while pending or active:
                while pending and len(active) < max_active and (credit > 0 or not active):
                    active.append(pending.pop(0))
                    credit = max(0, credit - 1)
                for g in list(active):
                    try:
                        r = next(g)
                        if r == "spawn":
                            credit += 1
                    except StopIteration:
                        active.remove(g)

        qhT = TAb(0, 4096).rearrange("p (c t) -> p c t", t=TT)
        khT = TAb(4096, 2048).rearrange("p (c t) -> p c t", t=TT)
        VX = TAb(6144, 1040).rearrange("p (n h d) -> p n h d", h=4, d=65)
        ssq8 = TAf(7184, 8)
        rs8 = TAf(7192, 8)
        den4 = [TAf(7200, 4), TAf(7204, 4)]
        BC_qhT = [[b_ta[c * 8 + t] for t in range(8)] for c in range(2)]
        BC_khT = [b_ta[16 + t] for t in range(8)]
        BC_VX = [b_ta[24 + t] for t in range(8)]
        BC_s8, BC_r8 = b_ta[32], b_ta[33]
        BC_den = [b_ta[34], b_ta[35]]
        sqf = M[:, 0:512]
        qn = M[:, 512:1024]
        qhb = M[:, 1024:1280].bitcast(BF16)
        kdup = M[:, 1280:1536].bitcast(BF16).rearrange("p (h r d) -> p h r d", r=2, d=64)
        EX = [M[:, 1536:1792].bitcast(BF16), M[:, 1792:2048].bitcast(BF16)]
        PT = [[M[:, 2048 + (i * 2 + kb) * 256:2048 + (i * 2 + kb + 1) * 256].bitcast(BF16) for kb in range(2)] for i in range(2)]
        obt = [M[:, 3072:3584].bitcast(BF16), M[:, 3584:4096].bitcast(BF16)]
        BM_sqf, BM_qn, BM_qhb, BM_kdup = b_m[0], b_m[1], b_m[2], b_m[3]
        BM_EX = [b_m[4], b_m[5]]
        BM_PT = [[b_m[6], b_m[7]], [b_m[8], b_m[9]]]
        BM_obt = [b_m[10], b_m[11]]

        def head_norm(k, c0, nh, gain_bc, out_bf, bout):
            n = nh * 64
            src = PS[k][:, c0:c0 + n]
            rb = bqs(k, c0, c0 + n)
            S.op("act", lambda e: e.activation(out=sqf[:, 0:n], in_=src, func=AF.Square), reads=rb, writes=[BM_sqf])
            S.op("dve", lambda e: e.tensor_reduce(out=ssq8[:, 0:nh], in_=sqf[:, 0:n].rearrange("p (h d) -> p h d", d=64), axis=AX.X, op=ALU.add), reads=[BM_sqf], writes=[BC_s8])
            rstd_ops(ssq8[:, 0:nh], rs8[:, 0:nh], 1.0 / 64, BC_s8, BC_r8)
            S.op("dve", lambda e: e.tensor_tensor(out=qn[:, 0:n].rearrange("p (h d) -> p h d", d=64), in0=src.rearrange("p (h d) -> p h d", d=64),
                                                  in1=rs8[:, 0:nh].unsqueeze(2).broadcast_to([128, nh, 64]), op=ALU.mult), reads=rb + [BC_r8], writes=[BM_qn])
            for (oap, extra) in out_bf:
                S.op("dve", lambda e, oap=oap: e.tensor_tensor(out=oap, in0=qn[:, 0:n].rearrange("p (h d) -> p h d", d=64),
                                                                in1=gain_bc.broadcast_to([128, nh, 64]), op=ALU.mult), reads=[BM_qn, b_small, b_const], writes=[bout])

        def kv_stages(wv, wbuf, t, dstK, dstV, bK, bV):
            st_ = {}

            def A():
                st_["k"] = nb()
                mm_group(st_["k"], 0, [(HT[:, kc, t * 128:(t + 1) * 128], wv[:, kc, :]) for kc in range(KC)], [wbuf] + hreads([t]))

            def Bs():
                k = st_["k"]
                head_norm(k, 0, 4, kgb.unsqueeze(1), [(kdup[:, :, 0, :], None), (kdup[:, :, 1, :], None)], BM_kdup)
                S.op("dve", lambda e: e.tensor_copy(out=dstV, in_=PS[k][:, 256:512].rearrange("p (h d) -> p h d", d=64)), reads=bqs(k), writes=[bV])

            def Cs():
                kz = nb()

                def trs(pe):
                    ins = None
                    for hk in range(4):
                        ins = pe.transpose(PSB[kz][:, hk * 128:(hk + 1) * 128], kdup[:, hk, :, :].rearrange("p r d -> p (r d)"), IDB)
                    return ins
                S.op("pe", trs, reads=[BM_kdup, b_const], writes=bqs(kz))
                S.op("act", lambda e: e.activation(out=dstK, in_=PSB[kz][:, 0:512].rearrange("p (h t) -> p h t", t=128), func=AF.Copy), reads=bqs(kz), writes=[bK])
            return [A, Bs, Cs]

        def kv_tile(wv, wbuf, t, dstK, dstV, bK, bV):
            for f in kv_stages(wv, wbuf, t, dstK, dstV, bK, bV):
                f()

        def q_stages(wv, wbuf, pc, t):
            st_ = {}

            def A():
                st_["k"] = nb()
                mm_group(st_["k"], 0, [(HT[:, kc, t * 128:(t + 1) * 128], wv[:, kc, :]) for kc in range(KC)], [wbuf] + hreads([t]))

            def Bs():
                head_norm(st_["k"], 0, 8, qgb, [(qhb.rearrange("p (h d) -> p h d", d=64), None)], BM_qhb)

            def Cs():
                kz = nb()

                def trs(pe):
                    ins = None
                    for c in range(4):
                        ins = pe.transpose(PSB[kz][:, c * 128:(c + 1) * 128], qhb[:, c * 128:(c + 1) * 128], IDB)
                    return ins
                S.op("pe", trs, reads=[BM_qhb, b_const], writes=bqs(kz))
                S.op("act", lambda e: e.activation(out=qhT[:, pc * 4:(pc + 1) * 4, t * 128:(t + 1) * 128],
                                                   in_=PSB[kz][:, 0:512].rearrange("p (c t) -> p c t", t=128), func=AF.Copy),
                     reads=bqs(kz), writes=[BC_qhT[pc][t]])
            return [A, Bs, Cs]

        def swa_phase(first_group):
            S.op("dve", lambda e: e.memset(VX[:, :, :, 64:65], 1.0), writes=BC_VX)
            items = []
            wq = [wload([(w_in[:, C_QB + pc * 512:C_QB + (pc + 1) * 512], 0)], 512) for pc in range(2)]
            wkv = wload([(w_in[:, C_KB:C_KB + 512], 0)], 512)
            for pc in range(2):
                for t in range(NT):
                    items.append(q_stages(wq[pc][0], wq[pc][1], pc, t))
            for t in range(NT):
                items.append(kv_stages(wkv[0], wkv[1], t, khT[:, :, t * 128:(t + 1) * 128], VX[:, t, :, 0:64], BC_khT[t], BC_VX[t]))
            pipeline(items)

            def blk_stages(n, hk):
                i3 = n % 2
                i2 = hk % 2

                def Ss():
                    for kb in range(2):
                        ksa, ksb = nb(), nb()
                        halo = (kb == 0 and n == 0)

                        def sc(pe, ksa=ksa, ksb=ksb, kb=kb, halo=halo):
                            ins = None
                            for g in range(4):
                                base = (g % 2) * 64
                                ch = 2 * hk + g // 2
                                if halo:
                                    l = KH[base:base + 64, hk, :]
                                else:
                                    blk = n - 1 + kb
                                    l = khT[base:base + 64, hk, blk * 128:(blk + 1) * 128]
                                r = qhT[base:base + 64, ch, n * 128:(n + 1) * 128]
                                kk_ = ksa if g % 2 == 0 else ksb
                                ins = pe.matmul(PS[kk_][:, (g // 2) * 128:(g // 2 + 1) * 128], lhsT=l, rhs=r, start=True, stop=True)
                            return ins
                        rd = [BC_qhT[(2 * hk) // 4][n]]
                        rd.append(b_halo if halo else BC_khT[n - 1 + kb])
                        S.op("pe", sc, reads=rd, writes=bqs(ksa) + bqs(ksb))
                        exv = EX[kb].rearrange("p (a b q) -> p a b q", b=2, q=128)

                        def exf(e, ksa=ksa, ksb=ksb, exv=exv):
                            e.activation(out=exv[:, :, 0, :], in_=PS[ksa][:, 0:256].rearrange("p (a q) -> p a q", q=128), func=AF.Exp)
                            return e.activation(out=exv[:, :, 1, :], in_=PS[ksb][:, 0:256].rearrange("p (a q) -> p a q", q=128), func=AF.Exp)
                        S.op("act", exf, reads=bqs(ksa) + bqs(ksb), writes=[BM_EX[kb]])
                        if kb == 1:
                            mk = MCm
                        elif halo and first_group:
                            mk = MP0[:]
                        else:
                            mk = MPm
                        S.op("dve", lambda e, kb=kb, mk=mk: e.tensor_tensor(out=PT[i2][kb].rearrange("p (g q) -> p g q", q=128), in0=EX[kb].rearrange("p (g q) -> p g q", q=128),
                                                                         in1=mk.unsqueeze(1).broadcast_to([128, 4, 128]), op=ALU.mult),
                             reads=[BM_EX[kb], b_const], writes=[BM_PT[i2][kb]])

                def Ps():
                    ko = nb()

                    def pv(pe):
                        ins = None
                        for g in range(4):
                            vp = VH[:, hk, :] if n == 0 else VX[:, n - 1, hk, :]
                            pe.matmul(PS[ko][:, g * 65:(g + 1) * 65], lhsT=PT[i2][0][:, g * 128:(g + 1) * 128], rhs=vp, start=True, stop=False)
                            ins = pe.matmul(PS[ko][:, g * 65:(g + 1) * 65], lhsT=PT[i2][1][:, g * 128:(g + 1) * 128], rhs=VX[:, n, hk, :], start=False, stop=True)
                        return ins
                    S.op("pe", pv, reads=[BM_PT[i2][0], BM_PT[i2][1], BC_VX[n], (b_halo if n == 0 else BC_VX[n - 1])], writes=bqs(ko))
                    ov = PS[ko][:, 0:260].rearrange("p (g c) -> p g c", c=65)
                    S.op("dve", lambda e: e.tensor_tensor(out=den4[i2].unsqueeze(2), in0=ov[:, :, 64:65], in1=esink[:, hk * 4:(hk + 1) * 4].unsqueeze(2), op=ALU.add),
                         reads=bqs(ko) + [b_small], writes=[BC_den[i2]])
                    S.op("dve", lambda e: e.reciprocal(out=den4[i2], in_=den4[i2]), reads=[BC_den[i2]], writes=[BC_den[i2]])
                    S.op("dve", lambda e: e.tensor_tensor(out=obt[i3][:, hk * 256:(hk + 1) * 256].rearrange("p (g d) -> p g d", d=64), in0=ov[:, :, 0:64],
                                                          in1=den4[i2].unsqueeze(2).broadcast_to([128, 4, 64]), op=ALU.mult),
                         reads=bqs(ko) + [BC_den[i2]], writes=[BM_obt[i3]])
                    if hk == 3:
                        kz = nb()

                        def trs(pe):
                            ins = None
                            for c in range(8):
                                ins = pe.transpose(PSB[kz][:, c * 128:(c + 1) * 128], obt[i3][:, c * 128:(c + 1) * 128], IDB)
                            return ins
                        S.op("pe", trs, reads=[BM_obt[i3], b_const], writes=bqs(kz))
                        S.op("act", lambda e: e.activation(out=obT[:, :, n * 128:(n + 1) * 128], in_=PSB[kz][:, 0:1024].rearrange("p (c t) -> p c t", t=128), func=AF.Copy),
                             reads=bqs(kz), writes=[b_obT[n]])
                return [Ss, Ps]
            pipeline([blk_stages(n, hk) for n in range(NT) for hk in range(4)])
            S.op("dve", lambda e: e.tensor_copy(out=KH[:], in_=khT[:, :, 7 * 128:8 * 128]), reads=[BC_khT[7]], writes=[b_halo])
            S.op("dve", lambda e: e.tensor_copy(out=VH[:, :, 0:64], in_=VX[:, 7, :, 0:64]), reads=[BC_VX[7]], writes=[b_halo])

        t1s = TAf(0, 4096).rearrange("p (c t) -> p c t", t=TT)
        sgst = [TAf(4096, 512), TAf(4608, 512)]
        t2st = [TAf(5120, 512), TAf(5632, 512)]
        BD_t1 = [[b_ta[c * 2 + hf] for hf in range(2)] for c in range(4)]
        BD_sg = [b_ta[8], b_ta[9]]
        BD_t2 = [b_ta[10], b_ta[11]]

        def wload2(src_g, src_y):
            i = wslot()
            vg = W[i][:, 0:4096].rearrange("p (a b) -> p a b", b=256)
            vy = W[i][:, 4096:6144].rearrange("p (a b) -> p a b", b=256)
            S.op("pool", lambda e: e.dma_start(out=vg, in_=src_g.rearrange("(kc p) n -> p kc n", p=128)), writes=[wb[i]], dma=wb[i])
            S.op("pool", lambda e: e.dma_start(out=vy, in_=src_y.rearrange("(kc p) n -> p kc n", p=128)), writes=[wb[i]], dma=wb[i])
            return vg, vy, wb[i]

        def merge_phase():
            cnt = 0
            for cq in range(4):
                for br in range(2):
                    wsrc = w_a if br == 0 else w_b
                    oT = oaT if br == 0 else obT
                    for h2 in range(2):
                        gcol = (C_GATEA if br == 0 else C_GATEB) + cq * 512 + h2 * 256
                        wg, wy, wbuf_ = wload2(w_in[:, gcol:gcol + 256], wsrc[:, cq * 512 + h2 * 256:cq * 512 + (h2 + 1) * 256])
                        for c2 in range(2):
                            c4 = h2 * 2 + c2
                            c = cq * 4 + c4
                            for hf in range(2):
                                i = cnt % 2
                                cnt += 1
                                kg_ = nb()
                                mm_group(kg_, 0, [(wg[:, kc, c2 * 128:(c2 + 1) * 128], HT[:, kc, hf * 512:(hf + 1) * 512]) for kc in range(KC)],
                                         [wbuf_] + hreads(range(hf * 4, hf * 4 + 4)))
                                ky_ = nb()
                                if br == 0:
                                    rdo = [b_oaT[kc][t] for kc in range(8) for t in range(hf * 4, hf * 4 + 4)]
                                else:
                                    rdo = [b_obT[t] for t in range(hf * 4, hf * 4 + 4)]
                                mm_group(ky_, 0, [(wy[:, kc, c2 * 128:(c2 + 1) * 128], oT[:, kc, hf * 512:(hf + 1) * 512]) for kc in range(8)], [wbuf_] + rdo)
                                S.op("act", lambda e, kg_=kg_, i=i: e.activation(out=sgst[i], in_=PS[kg_][:, 0:512], func=AF.Sigmoid), reads=bqs(kg_), writes=[BD_sg[i]])
                                if br == 0:
                                    S.op("dve", lambda e, ky_=ky_, i=i, c4=c4, hf=hf: e.tensor_tensor(out=t1s[:, c4, hf * 512:(hf + 1) * 512], in0=PS[ky_][:, 0:512], in1=sgst[i], op=ALU.mult),
                                         reads=bqs(ky_) + [BD_sg[i]], writes=[BD_t1[c4][hf]])
                                else:
                                    S.op("dve", lambda e, ky_=ky_, i=i: e.tensor_tensor(out=t2st[i], in0=PS[ky_][:, 0:512], in1=sgst[i], op=ALU.mult),
                                         reads=bqs(ky_) + [BD_sg[i]], writes=[BD_t2[i]])
                                    S.op("dve", lambda e, i=i, c=c, c4=c4, hf=hf: e.tensor_tensor(out=mergedT[:, c, hf * 512:(hf + 1) * 512], in0=t1s[:, c4, hf * 512:(hf + 1) * 512], in1=t2st[i], op=ALU.add),
                                         reads=[BD_t2[i], BD_t1[c4][hf]], writes=[b_mT[c][hf]])

        HTf = HT[:].rearrange("p a b -> p (a b)").bitcast(F32)
        GT1 = HTf[:, 0:2048]
        tmpE = [HTf[:, 2048:2560], HTf[:, 2560:3072]]

        def outproj_phase(tok0):
            for t in range(NT):
                S.op("sp", lambda e, t=t: e.dma_start(out=xres[:, t, :], in_=x_own[tok0 + t * 128:tok0 + (t + 1) * 128, :]), writes=b_xres[t], dma=b_xres[t][0])
            S.op("sp", lambda e: e.dma_start(out=GT1, in_=mod_d[0, 2 * D:3 * D].partition_broadcast(128)), reads=b_modd2p, writes=[b_hte[0]], dma=b_hte[0])
            cnt = 0
            for pc in range(4):
                wv, wbuf = wload([(w_out[:, pc * 512:(pc + 1) * 512], 0)], 512)
                for t in range(NT):
                    i = cnt % 2
                    cnt += 1
                    k = nb()
                    mm_group(k, 0, [(mergedT[:, kc, t * 128:(t + 1) * 128], wv[:, kc, :]) for kc in range(KC)],
                             [wbuf] + [b_mT[kc][t // 4] for kc in range(KC)])
                    S.op("dve", lambda e, k=k, i=i, pc=pc: e.tensor_tensor(out=tmpE[i], in0=PS[k][:, 0:512], in1=GT1[:, pc * 512:(pc + 1) * 512], op=ALU.mult),
                         reads=bqs(k) + [b_hte[0]], writes=[b_hte[1 + i]])
                    S.op("dve", lambda e, i=i, t=t, pc=pc: e.tensor_tensor(out=xres[:, t, pc * 512:(pc + 1) * 512], in0=xres[:, t, pc * 512:(pc + 1) * 512], in1=tmpE[i], op=ALU.add),
                         reads=[b_hte[1 + i], b_xres[t][0], b_xres[t][pc]], writes=[b_xres[t][pc]])

        rl = [STG[:, 0:512], STG[:, 512:1024]]
        tmpG = [STG[:, 1024:1536]]

        def mlp_phase(tok0):
            cnt = 0
            for hb in range(4):
                for w1p in range(4):
                    wv, wbuf = wload([(w1[:, hb * 2048 + w1p * 512:hb * 2048 + (w1p + 1) * 512], 0)], 512)
                    for c4 in range(4):
                        hc = w1p * 4 + c4
                        for hf in range(2):
                            i = cnt % 2
                            cnt += 1
                            k = nb()
                            mm_group(k, 0, [(wv[:, kc, c4 * 128:(c4 + 1) * 128], HT[:, kc, hf * 512:(hf + 1) * 512]) for kc in range(KC)],
                                     [wbuf] + hreads(range(hf * 4, hf * 4 + 4)))
                            S.op("act", lambda e, k=k, i=i: e.activation(out=rl[i], in_=PS[k][:, 0:512], func=AF.Relu), reads=bqs(k), writes=[b_stg[i]])
                            S.op("dve", lambda e, i=i, hc=hc, hf=hf: e.tensor_tensor(out=hidT[:, hc, hf * 512:(hf + 1) * 512], in0=rl[i], in1=rl[i], op=ALU.mult),
                                 reads=[b_stg[i]], writes=[b_hid[hc][hf]])
                for cq in range(4):
                    wv, wbuf = wload([(w2[hb * 2048:(hb + 1) * 2048, cq * 512:(cq + 1) * 512], 0)], 512)
                    for t in range(NT):
                        k = nb()
                        mm_group(k, 0, [(hidT[:, hc, t * 128:(t + 1) * 128], wv[:, hc, :]) for hc in range(16)],
                                 [wbuf] + [b_hid[hc][t // 4] for hc in range(16)])
                        S.op("dve", lambda e, k=k, cq=cq: e.tensor_tensor(out=tmpG[0], in0=PS[k][:, 0:512], in1=GT2[:, cq * 512:(cq + 1) * 512], op=ALU.mult),
                             reads=bqs(k) + [b_p2b], writes=[b_stg[2]])
                        S.op("dve", lambda e, t=t, cq=cq: e.tensor_tensor(out=xres[:, t, cq * 512:(cq + 1) * 512], in0=xres[:, t, cq * 512:(cq + 1) * 512], in1=tmpG[0], op=ALU.add),
                             reads=[b_stg[2], b_xres[t][cq]], writes=[b_xres[t][cq]])
            for t in range(NT):
                S.op("sp", lambda e, t=t: e.dma_start(out=out_d[tok0 + t * 128:tok0 + (t + 1) * 128, :], in_=xres[:, t, :]), reads=b_xres[t], writes=[b_out], dma=b_out)

        XT_A = [TAf(0, 2048), TAf(2048, 2048)]
        JK_A = TAb(4096, 1024)
        XT_F = [M[:, 0:2048], M[:, 2048:4096]]
        JK_F = M[:, 4096:5120].bitcast(BF16)
        all_x = [b_oaT, b_obT, b_xres, b_ta]
        all_m = [b_m, b_mT, b_hid]

        def load_p2a():
            S.op("sp", lambda e: e.dma_start(out=LNOML, in_=lnoml_d[0, :].partition_broadcast(128)), reads=[b_lnd], writes=[b_p2a], dma=b_p2a)
            S.op("sp", lambda e: e.dma_start(out=OG, in_=ogain_d[0, :].partition_broadcast(128)), writes=[b_p2a], dma=b_p2a)

        def program():
            stop = CFG.get('stop')
            if stop == 'setup':
                return
            fence([all_x, all_m, b_p2a, b_p2b])
            load_p2a()
            if CFG["prefix"]:
                for pg in range(2):
                    norm_phase(x_pre, pg * TT, g1c, sh1c, XT_A, JK_A, [b_ta[40], b_ta[41], b_ta[42]])
                    fence([b_ta, b_m])
                    wl_ = {}
                    run_interleaved([hgrn_gen(j, False, j % 4, extra=(ada_extra(pg, j) if pg == 0 else None), wl=wl_) for j in range(NH)], max_active=4)
                    if pg == 1:
                        wv, wbuf = wload([(w_in[:, C_KB:C_KB + 512], 0)], 512)
                        fence([b_ta, b_m])
                        kv_tile(wv, wbuf, 7, KH[:], VH[:, :, 0:64], b_halo, b_halo)
                    else:
                        load_mod2()
                    fence([b_ta, b_m])
                S.op("dve", lambda e: e.tensor_scalar(out=ST[:].rearrange("p a b -> p (a b)"), in0=ST[:].rearrange("p a b -> p (a b)"), scalar1=flagt, scalar2=None, op0=ALU.mult),
                     reads=b_ST + [b_const], writes=b_ST)
            if DBG:
                S.op("sp", lambda e: e.dma_start(out=dbg["st"], in_=ST[:]), reads=b_ST, writes=[b_dbg], dma=b_dbg)

            for g in range(NG):
                tok0 = g * TT
                norm_phase(x_own, tok0, g1c, sh1c, XT_A, JK_A, [b_ta[40], b_ta[41], b_ta[42]])
                if stop == 'normA':
                    S.op('sp', lambda e: e.dma_start(out=dbg['hT'], in_=HT[:]), reads=hreads(range(NT)), writes=[b_dbg], dma=b_dbg)
                    return
                if DBG and g == 0:
                    S.op("sp", lambda e: e.dma_start(out=dbg["hT"], in_=HT[:]), reads=hreads(range(NT)), writes=[b_dbg], dma=b_dbg)
                fence([b_ta, b_m])
                wl_ = {}
                run_interleaved([hgrn_gen(j, True, j % 2, wl=wl_) for j in range(CFG.get('nheads', NH))])
                if stop == 'hgrn':
                    S.op('sp', lambda e: e.dma_start(out=dbg['oaT'], in_=oaT), reads=flat(b_oaT), writes=[b_dbg], dma=b_dbg)
                    return
                fence([b_ta, all_m])
                swa_phase(first_group=(g == 0))
                if stop == 'swa':
                    S.op('sp', lambda e: e.dma_start(out=dbg['obT'], in_=obT), reads=flat(b_obT), writes=[b_dbg], dma=b_dbg)
                    return
                if DBG and g == 0:
                    S.op("sp", lambda e: e.dma_start(out=dbg["oaT"], in_=oaT), reads=flat(b_oaT), writes=[b_dbg], dma=b_dbg)
                    S.op("sp", lambda e: e.dma_start(out=dbg["obT"], in_=obT), reads=flat(b_obT), writes=[b_dbg], dma=b_dbg)
                fence([b_ta, all_m])
                merge_phase()
                if DBG and g == 0:
                    S.op("sp", lambda e: e.dma_start(out=dbg["mT"], in_=mergedT), reads=flat(b_mT), writes=[b_dbg], dma=b_dbg)
                if stop == 'merge':
                    return
                fence([all_x, b_h, b_hte])
                outproj_phase(tok0)
                if DBG and g == 0:
                    S.op("sp", lambda e: e.dma_start(out=dbg["x1"], in_=xres), reads=flat(b_xres), writes=[b_dbg], dma=b_dbg)
                if stop == 'outproj':
                    return
                fence([all_m, b_h, b_hte, b_p2a, b_p2b, b_stg])
                S.op("sp", lambda e: e.dma_start(out=GT2, in_=mod_d[0, 5 * D:6 * D].partition_broadcast(128)), reads=b_modd2p, writes=[b_p2b], dma=b_p2b)
                norm_phase(None, 0, g2c, sh2c, XT_F, JK_F, [b_m[40], b_m[41], b_m[42]], xres_src=True)
                fence([all_m])
                mlp_phase(tok0)
                if g + 1 < NG:
                    fence([all_x, all_m, b_p2a, b_p2b, b_stg])
                    load_p2a()


        program()
        fw = ([b_out] if b_out.dma_cnt else []) + ([b_dbg] if (DBG and b_dbg.dma_cnt) else [])
        S.emit(final_wait_bufs=fw)
    return nc


def make_in_maps(inp):
    x = np.asarray(inp["x"], np.float32)
    c = np.asarray(inp["c"], np.float32)
    cf, cb, MP = _consts()
    f32 = lambda a: np.ascontiguousarray(np.asarray(a, np.float32))
    shared = {
        "w_ada": f32(inp["w_ada"][0]), "b_ada": f32(inp["b_ada"][0]).reshape(1, -1),
        "n1c": f32(np.asarray(inp["norm1_gain"][0]).reshape(16, 128).T),
        "n2c": f32(np.asarray(inp["norm2_gain"][0]).reshape(16, 128).T),
        "w_in": f32(inp["w_in"][0]), "lb_logits": f32(inp["lb_logits"]),
        "hgrn_o_gain": f32(inp["hgrn_o_gain"][0]).reshape(1, -1),
        "q_norm_gain": f32(inp["q_norm_gain"][0]).reshape(1, -1),
        "k_norm_gain": f32(inp["k_norm_gain"][0]).reshape(1, -1),
        "sinks": f32(inp["sinks"][0]).reshape(1, -1),
        "w_branch_a": f32(inp["w_branch_a"][0]), "w_branch_b": f32(inp["w_branch_b"][0]),
        "w_out": f32(inp["w_out"][0]), "w_mlp_in": f32(inp["w_mlp_in"][0]), "w_mlp_out": f32(inp["w_mlp_out"][0]),
        "cf": cf, "cb": cb,
    }
    zeros_x = np.zeros((TCORE, D), np.float32)
    maps = []
    for core in range(8):
        b, hf = core // 2, core % 2
        m = dict(shared)
        m["x_own"] = np.ascontiguousarray(x[b, hf * TCORE:(hf + 1) * TCORE])
        m["x_pre"] = np.ascontiguousarray(x[b, 0:TCORE]) if hf == 1 else zeros_x
        m["flag"] = np.full((128, 1), float(hf), np.float32)
        m["c_col"] = np.ascontiguousarray(c[b].reshape(16, 128).T)
        m["mp0"] = MP if hf == 1 else np.zeros_like(MP)
        maps.append(m)
    return maps


def kernel(**inp):
    nc = build_program()
    maps = make_in_maps(inp)
    res = run_bass_kernel_spmd(nc, maps, core_ids=list(range(8)))
    out = np.zeros((4, 4096, D), np.float32)
    for core in range(8):
        b, hf = core // 2, core % 2
        out[b, hf * TCORE:(hf + 1) * TCORE] = np.asarray(res.results[core]["out"], np.float32)
    kernel.last_results = res
    return out
```

```python
import numpy as np
import ml_dtypes
from contextlib import ExitStack
import concourse.bass as bass
import concourse.mybir as mybir
from concourse.bass_utils import run_bass_kernel_spmd

F32 = mybir.dt.float32
BF16 = mybir.dt.bfloat16
AF = mybir.ActivationFunctionType
ALU = mybir.AluOpType
AX = mybir.AxisListType

ENGS = ("pe", "act", "dve", "pool", "sp")
EPS = 1e-6


class Buf:
    __slots__ = ("name", "last_w", "readers", "dma_sem", "dma_cnt", "psum")

    def __init__(self, name, psum=False):
        self.name = name
        self.psum = psum
        self.last_w = None
        self.readers = []
        self.dma_sem = None
        self.dma_cnt = 0


class Op:
    __slots__ = ("eng", "fn", "idx", "deps", "signal", "sigval", "is_dma",
                 "dma_buf", "dma_val", "waits")


class Sched:
    def __init__(self, nc):
        self.nc = nc
        self.ops = {e: [] for e in ENGS}
        self.dma_bufs = []

    def buf(self, name="b", psum=False):
        return Buf(name, psum)

    def op(self, eng, fn, reads=(), writes=(), dma=None):
        o = Op()
        o.eng, o.fn = eng, fn
        o.idx = len(self.ops[eng])
        o.signal = False
        o.sigval = None
        o.is_dma = dma is not None
        o.dma_buf = dma
        o.dma_val = 0
        o.waits = None
        if dma is not None:
            if dma.dma_sem is None:
                dma.dma_sem = True
                self.dma_bufs.append(dma)
            dma.dma_cnt += 16
            o.dma_val = dma.dma_cnt
        pr = [b for b in reads if b.psum]
        if pr:
            reads = [b for b in reads if not b.psum]
            writes = list(writes) + [b for b in pr if b not in writes]
        deps = []
        for b in reads:
            if b.last_w is not None:
                deps.append(b.last_w)
        for b in writes:
            w = b.last_w
            if w is not None:
                if w.eng == eng and not w.is_dma:
                    pass
                elif not (o.is_dma and w.is_dma and w.dma_buf is o.dma_buf):
                    deps.append(w)
            deps.extend(r for r in b.readers if r.eng != eng or r.is_dma)
        for b in reads:
            b.readers.append(o)
        for b in writes:
            b.last_w = o
            b.readers = []
        red = {}
        for d in deps:
            if d is o:
                continue
            if d.is_dma:
                k = ("d", id(d.dma_buf))
                if k not in red or red[k].dma_val < d.dma_val:
                    red[k] = d
            else:
                if d.eng == "pe" and eng == "pe":
                    continue
                k = ("e", d.eng)
                if k not in red or red[k].idx < d.idx:
                    red[k] = d
        o.deps = list(red.values())
        for d in o.deps:
            if d.is_dma and d.dma_val < d.dma_buf.dma_cnt:
                print("SCHED WARNING: partial DMA-semaphore wait on", d.dma_buf.name, d.dma_val, d.dma_buf.dma_cnt)
        self.ops[eng].append(o)
        return o

    def emit(self, final_wait_bufs=()):
        nc = self.nc
        for e in ENGS:
            for o in self.ops[e]:
                for d in o.deps:
                    if not d.is_dma:
                        d.signal = True
        for e in ENGS:
            c = 0
            for o in self.ops[e]:
                if o.signal and not o.is_dma:
                    c += 1
                    o.sigval = c
        with ExitStack() as st:
            esem = {e: st.enter_context(nc.semaphore("s_" + e)) for e in ENGS}
            for i, b in enumerate(self.dma_bufs):
                b.dma_sem = st.enter_context(nc.semaphore("d%d" % i))
            for e in ENGS:
                known = {}
                for o in self.ops[e]:
                    need = {}
                    for d in o.deps:
                        if d.is_dma:
                            key = ("d", id(d.dma_buf))
                            sem, val = d.dma_buf.dma_sem, d.dma_val
                        else:
                            if d.sigval is None:
                                continue
                            key = ("e", d.eng)
                            sem, val = esem[d.eng], d.sigval
                        if known.get(key, 0) >= val:
                            continue
                        if key not in need or need[key][1] < val:
                            need[key] = (sem, val)
                    for k, (sem, val) in need.items():
                        known[k] = val
                    o.waits = list(need.values())
            block = st.enter_context(nc.Block())

            def run(engname, eng):
                for o in self.ops[engname]:
                    for sem, val in o.waits:
                        eng.wait_ge(sem, val)
                    ins = o.fn(eng)
                    if o.is_dma:
                        ins.then_inc(o.dma_buf.dma_sem, 16)
                    elif o.signal:
                        ins.then_inc(esem[engname], 1)
                if engname == "sp":
                    for b in final_wait_bufs:
                        eng.wait_ge(b.dma_sem, b.dma_cnt)

            @block.tensor
            def _(pe):
                run("pe", pe)

            @block.scalar
            def _(act):
                run("act", act)

            @block.vector
            def _(dve):
                run("dve", dve)

            @block.gpsimd
            def _(pool):
                run("pool", pool)

            @block.sync
            def _(sp):
                run("sp", sp)


D = 2048
TCORE = 2048
TT = 1024
NT = TT // 128
KC = 16
NH = 8
C_QA, C_FA, C_IA, C_GA = 0, 1024, 2048, 3072
C_QB, C_KB, C_VB = 4096, 5120, 5376
C_GATEA, C_GATEB = 5632, 7680
NCF = 132 + 128 * 4 + 2

CFG = {"ng": 2, "prefix": True, "debug": False}


def _consts():
    s = np.arange(128)[:, None]
    t = np.arange(128)[None, :]
    same = (s // 64) == (t // 64)
    mid = (t // 64) * 64 + 31
    MA = np.zeros((128, 132), np.float32)
    MA[:, :128] = same * ((s <= t).astype(np.float32) - (s <= mid).astype(np.float32))
    for c in range(2):
        MA[:, 128 + c] = ((s[:, 0] // 64) == c) * (s[:, 0] <= c * 64 + 31)
        MA[:, 130 + c] = ((s[:, 0] // 64) == c)
    NM1 = -MA[:, :128]
    IDF = np.eye(128, dtype=np.float32)
    M2 = (same * (s > t)).astype(np.float32)
    MASKH = (same * (s <= t)).astype(np.float32)
    RM = np.stack([(np.arange(128) < 64), (np.arange(128) >= 64)], axis=1).astype(np.float32)
    cf = np.concatenate([MA, NM1, IDF, M2, MASKH, RM], axis=1).astype(np.float32)
    MC = (s <= t).astype(np.float32)
    MP = (s > t).astype(np.float32)
    cb = np.concatenate([IDF, MC, MP], axis=1).astype(ml_dtypes.bfloat16)
    return cf, cb, MP.astype(ml_dtypes.bfloat16)


def build_program():
    nc = bass.Bass("TRN2", target_bir_lowering=False)
    NG = CFG["ng"]
    DBG = CFG["debug"]

    def din(name, shape, dt=F32):
        return nc.dram_tensor(name, shape, dt, kind="ExternalInput").ap()

    x_own = din("x_own", [TCORE, D])
    x_pre = din("x_pre", [TCORE, D])
    flag_d = din("flag", [128, 1])
    c_col = din("c_col", [128, 16])
    w_ada = din("w_ada", [D, 6 * D])
    b_ada = din("b_ada", [1, 6 * D])
    n1c_d = din("n1c", [128, 16])
    n2c_d = din("n2c", [128, 16])
    w_in = din("w_in", [D, 9728])
    lbl = din("lb_logits", [2, 1024])
    ogain_d = din("hgrn_o_gain", [1, 1024])
    qg_d = din("q_norm_gain", [1, 64])
    kg_d = din("k_norm_gain", [1, 64])
    sinks_d = din("sinks", [1, 16])
    w_a = din("w_branch_a", [1024, D])
    w_b = din("w_branch_b", [1024, D])
    w_out = din("w_out", [D, D])
    w1 = din("w_mlp_in", [D, 4 * D])
    w2 = din("w_mlp_out", [4 * D, D])
    cf_d = din("cf", [128, NCF])
    cb_d = din("cb", [128, 384], BF16)
    mp0_d = din("mp0", [128, 128], BF16)
    out_d = nc.dram_tensor("out", [TCORE, D], F32, kind="ExternalOutput").ap()
    mod_d = nc.dram_tensor("mod_d", [1, 6 * D], F32).ap()
    lnoml_d = nc.dram_tensor("lnoml_d", [1, 1024], F32).ap()
    dbg = {}
    if DBG:
        dbg["hT"] = nc.dram_tensor("dbg_hT", [128, 16, TT], BF16, kind="ExternalOutput").ap()
        dbg["oaT"] = nc.dram_tensor("dbg_oaT", [128, 8, TT], BF16, kind="ExternalOutput").ap()
        dbg["obT"] = nc.dram_tensor("dbg_obT", [128, 8, TT], BF16, kind="ExternalOutput").ap()
        dbg["mT"] = nc.dram_tensor("dbg_mT", [128, 16, TT], BF16, kind="ExternalOutput").ap()
        dbg["x1"] = nc.dram_tensor("dbg_x1", [128, 8, D], F32, kind="ExternalOutput").ap()
        dbg["mod"] = nc.dram_tensor("dbg_mod", [1, 6 * D], F32, kind="ExternalOutput").ap()
        dbg["st"] = nc.dram_tensor("dbg_st", [128, 8, 128], F32, kind="ExternalOutput").ap()

    S = Sched(nc)
    st = ExitStack()
    with st:
        def sb(name, shape, dt):
            return st.enter_context(nc.sbuf_tensor(name, shape, dt))

        X = sb("X", [128, 16384], F32)
        M = sb("M", [128, 8192], F32)
        HT = sb("HT", [128, 16, TT], BF16)
        W = [sb("W%d" % i, [128, 8192], BF16) for i in range(3)]
        CF = sb("CF", [128, NCF], F32)
        CB = sb("CB", [128, 384], BF16)
        MP0 = sb("MP0", [128, 128], BF16)
        ST = sb("ST", [128, 8, 128], F32)
        P2 = sb("P2", [128, 4096], F32)
        KH = sb("KH", [128, 4, 128], BF16)
        VH = sb("VH", [128, 4, 65], BF16)
        SM = sb("SM", [128, 256], F32)
        cT = sb("cT", [128, 16], BF16)
        PS = [st.enter_context(nc.psum_tensor("ps%d" % i, [128, 512], F32)) for i in range(8)]
        PSB = [p[:].bitcast(BF16) for p in PS]

        MA = CF[:, 0:132]
        NM1 = CF[:, 132:260]
        IDF = CF[:, 260:388]
        M2c = CF[:, 388:516]
        MASKH = CF[:, 516:644]
        RMc = CF[:, 644:646]
        IDB = CB[:, 0:128]
        MCm = CB[:, 128:256]
        MPm = CB[:, 256:384]

        g1c = SM[:, 0:16]
        sh1c = SM[:, 16:32]
        g2c = SM[:, 32:48]
        sh2c = SM[:, 48:64]
        esink = SM[:, 64:80]
        flagt = SM[:, 80:81]
        ccol = SM[:, 96:112]
        ctmp = SM[:, 112:128]
        tmpc = SM[:, 128:144]
        qgb = SM[:, 144:208].unsqueeze(1)
        kgb = P2[:, 4032:4096]
        stat = sb("stat", [128, 64], F32)

        LNOML = P2[:, 0:1024]
        OG = P2[:, 1024:2048]
        GT2 = P2[:, 0:2048]
        STG = P2[:, 2048:4032]

        b_const = S.buf("const")
        b_small = S.buf("small")
        b_modd = S.buf("modd")
        b_lnd = S.buf("lnd")
        wb = [S.buf("w%d" % i) for i in range(3)]
        bq = [S.buf("ps%d" % k, psum=True) for k in range(8)]
        b_h = [[S.buf("h%d_%d" % (kc, t)) for t in range(NT)] for kc in range(KC)]
        b_ST = [S.buf("st%d" % j) for j in range(NH)]
        b_halo = S.buf("halo")
        b_p2a = S.buf("p2a")
        b_p2b = S.buf("p2b")
        b_stg = [S.buf("stg%d" % i) for i in range(4)]
        b_stat = [S.buf("stat%d" % i) for i in range(16)]
        b_out = S.buf("out")
        b_dbg = S.buf("dbg")
        b_oaT = [[S.buf("oa") for t in range(NT)] for j in range(8)]
        b_obT = [S.buf("ob%d" % n) for n in range(NT)]
        b_xres = [[S.buf("xr") for pc in range(4)] for t in range(NT)]
        b_ta = [S.buf("ta%d" % i) for i in range(48)]
        b_m = [S.buf("m%d" % i) for i in range(48)]
        b_mT = [[S.buf("mT") for hf in range(2)] for c in range(16)]
        b_hid = [[S.buf("hid") for hf in range(2)] for c in range(16)]
        b_hte = [S.buf("hte%d" % i) for i in range(4)]
        b_fence = S.buf("fence")

        bank_ctr = [0]

        bank_pinned = set()

        def nb(pin=False):
            assert len(bank_pinned) < 8, "all PSUM banks pinned"
            while True:
                k = bank_ctr[0] % 8
                bank_ctr[0] += 1
                if k not in bank_pinned:
                    if pin:
                        bank_pinned.add(k)
                    return k

        def unpin(k):
            bank_pinned.discard(k)

        def bqs(k, lo=0, hi=512):
            return [bq[k]]

        def flat(ll):
            r = []
            for l in ll:
                if isinstance(l, (list, tuple)):
                    r.extend(flat(l))
                else:
                    r.append(l)
            return r

        def fence(bufs):
            bl = flat(bufs)
            S.op("dve", lambda e: e.memset(stat[:, 63:64], 0.0), writes=bl + [b_fence])

        wcnt = [0]
        wpinned = set()

        def wslot():
            while True:
                i = wcnt[0] % 3
                wcnt[0] += 1
                if i not in wpinned:
                    return i

        def wpin(buf):
            wpinned.add(wb.index(buf))

        def wunpin(buf):
            wpinned.discard(wb.index(buf))

        def wload(srcs, ncols, nk=16):
            i = wslot()
            view = W[i][:, 0:nk * ncols].rearrange("p (a b) -> p a b", b=ncols)
            for (src, off) in srcs:
                n = src.shape[1]
                S.op("pool", lambda e, src=src, off=off, n=n: e.dma_start(
                    out=view[:, :, off:off + n], in_=src.rearrange("(kc p) n -> p kc n", p=128)),
                    writes=[wb[i]], dma=wb[i])
            return view, wb[i]

        def mm_group(k, col0, pairs, reads, outrows=None, ncol=None):
            n = ncol if ncol is not None else pairs[0][1].shape[-1]
            oap = PS[k][:, col0:col0 + n] if outrows is None else PS[k][outrows[0]:outrows[1], col0:col0 + n]

            def fn(pe):
                ins = None
                L = len(pairs)
                for idx, (l, r) in enumerate(pairs):
                    ins = pe.matmul(oap, lhsT=l, rhs=r, start=(idx == 0), stop=(idx == L - 1))
                return ins
            S.op("pe", fn, reads=reads, writes=bqs(k, col0, col0 + n))

        def silu_chain(k, c0, n, stg_ap, stg_buf):
            src = PS[k][:, c0:c0 + n]
            rb = bqs(k, c0, c0 + n)
            S.op("act", lambda e: e.activation(out=stg_ap, in_=src, func=AF.Exp, scale=-1.0), reads=rb, writes=[stg_buf])
            S.op("act", lambda e: e.activation(out=stg_ap, in_=stg_ap, func=AF.Ln, bias=1.0), reads=[stg_buf], writes=[stg_buf])
            S.op("act", lambda e: e.activation(out=stg_ap, in_=stg_ap, func=AF.Exp, scale=-1.0), reads=[stg_buf], writes=[stg_buf])

        def rstd_ops(ss_ap, out_ap, n_inv, bss, bout):
            S.op("act", lambda e: e.activation(out=out_ap, in_=ss_ap, func=AF.Ln, scale=n_inv, bias=EPS), reads=[bss], writes=[bout])
            S.op("act", lambda e: e.activation(out=out_ap, in_=out_ap, func=AF.Exp, scale=-0.5), reads=[bout], writes=[bout])

        def ld(dst, src, buf, **kw):
            S.op("sp", lambda e: e.dma_start(out=dst, in_=src, **kw), writes=[buf], dma=buf)

        ld(CF[:], cf_d, b_const)
        ld(CB[:], cb_d, b_const)
        ld(MP0[:], mp0_d, b_const)
        ld(flagt, flag_d, b_const)
        ld(ccol, c_col, b_const)
        ld(tmpc, n1c_d, b_const)
        ld(SM[:, 224:240], n2c_d, b_const)
        n2c = SM[:, 224:240]
        ld(SM[:, 144:208], qg_d[0, :].partition_broadcast(128), b_const)
        ld(kgb, kg_d[0, :].partition_broadcast(128), b_const)
        ld(esink, sinks_d[0, :].partition_broadcast(128), b_const)
        S.op("dve", lambda e: e.memset(ST[:].rearrange("p a b -> p (a b)"), 0.0), writes=b_ST)
        S.op("dve", lambda e: e.memset(KH[:].rearrange("p a b -> p (a b)"), 0.0), writes=[b_halo])
        S.op("dve", lambda e: e.memset(VH[:].rearrange("p a b -> p (a b)"), 0.0), writes=[b_halo])
        S.op("dve", lambda e: e.memset(VH[:, :, 64:65], 1.0), writes=[b_halo])
        S.op("dve", lambda e: e.tensor_scalar(out=SM[:, 144:208], in0=SM[:, 144:208], scalar1=0.125, scalar2=None, op0=ALU.mult), reads=[b_const], writes=[b_small])
        S.op("act", lambda e: e.activation(out=esink, in_=esink, func=AF.Exp), reads=[b_const], writes=[b_small])
        S.op("act", lambda e: e.activation(out=ctmp, in_=ccol, func=AF.Exp, scale=-1.0), reads=[b_const], writes=[b_stat[0]])
        S.op("act", lambda e: e.activation(out=ctmp, in_=ctmp, func=AF.Ln, bias=1.0), reads=[b_stat[0]], writes=[b_stat[0]])
        S.op("act", lambda e: e.activation(out=ctmp, in_=ctmp, func=AF.Exp, scale=-1.0), reads=[b_stat[0]], writes=[b_stat[0]])
        S.op("dve", lambda e: e.tensor_tensor(out=cT[:], in0=ccol, in1=ctmp, op=ALU.mult), reads=[b_stat[0], b_const], writes=[b_small])
        adarow = sb("adarow", [1, 1024], F32)
        mrow = [STG[0:1, 0:512], STG[0:1, 512:1024]]
        brow = [adarow[0:1, 0:512], adarow[0:1, 512:1024]]
        b_brow = [S.buf("brow0"), S.buf("brow1")]
        b_moddp = [S.buf("modd_a"), S.buf("modd_b")]
        b_modd2p = [S.buf("modd2_a"), S.buf("modd2_b")]

        def ada_piece(pc):
            i = pc % 2
            mb = b_moddp[i] if pc < 8 else b_modd2p[i]
            S.op("sp", lambda e: e.dma_start(out=brow[i], in_=b_ada[:, pc * 512:(pc + 1) * 512]), writes=[b_brow[i]], dma=b_brow[i])
            wv, wbuf = wload([(w_ada[:, pc * 512:(pc + 1) * 512], 0)], 512)
            k = nb()
            mm_group(k, 0, [(cT[:, kc:kc + 1], wv[:, kc, :]) for kc in range(KC)], [wbuf, b_small], outrows=(0, 1))
            S.op("dve", lambda e: e.tensor_tensor(out=mrow[i], in0=PS[k][0:1, 0:512], in1=brow[i], op=ALU.add),
                 reads=bqs(k) + [b_brow[i]], writes=[b_stg[i]])
            S.op("sp", lambda e: e.dma_start(out=mod_d[:, pc * 512:(pc + 1) * 512], in_=mrow[i]), reads=[b_stg[i]], writes=[mb], dma=mb)

        def ada_extra(pg, j):
            def f():
                ada_piece(8 + 2 * j)
                ada_piece(9 + 2 * j)
            return f

        n_up = 8 if CFG["prefix"] else 24
        for pc in range(n_up):
            ada_piece(pc)

        def ldcol(dst, off, buf, src_buf):
            S.op("sp", lambda e: e.dma_start(out=dst, in_=mod_d[0, off:off + D].rearrange("(kc p) -> p kc", p=128), allow_slow_non_contiguous=True),
                 reads=list(src_buf), writes=[buf], dma=buf)

        b_mc = S.buf("modcols")
        b_mc2 = S.buf("modcols2")
        ldcol(sh1c, 0, b_mc, b_moddp)
        ldcol(g1c, D, b_mc, b_moddp)
        S.op("dve", lambda e: e.scalar_tensor_tensor(out=g1c, in0=g1c, scalar=1.0, in1=tmpc, op0=ALU.add, op1=ALU.mult), reads=[b_mc, b_const], writes=[b_small])

        def load_mod2():
            ldcol(sh2c, 3 * D, b_mc2, b_modd2p)
            ldcol(g2c, 4 * D, b_mc2, b_modd2p)
            S.op("dve", lambda e: e.scalar_tensor_tensor(out=g2c, in0=g2c, scalar=1.0, in1=n2c, op0=ALU.add, op1=ALU.mult), reads=[b_mc2, b_const], writes=[b_mc2])
        if not CFG["prefix"]:
            load_mod2()
        if DBG:
            pass
        lbt = M[:, 0:2048].rearrange("p (a b) -> p a b", b=1024)
        ld(lbt, lbl.partition_broadcast(128), b_m[0])
        S.op("dve", lambda e: e.tensor_tensor(out=lbt[:, 0, :], in0=lbt[:, 1, :], in1=lbt[:, 0, :], op=ALU.subtract), reads=[b_m[0]], writes=[b_m[0]])
        S.op("act", lambda e: e.activation(out=lbt[:, 0, :], in_=lbt[:, 0, :], func=AF.Exp, scale=-1.0), reads=[b_m[0]], writes=[b_m[0]])
        S.op("act", lambda e: e.activation(out=lbt[:, 0, :], in_=lbt[:, 0, :], func=AF.Ln, bias=1.0), reads=[b_m[0]], writes=[b_m[0]])
        S.op("dve", lambda e: e.tensor_scalar(out=lbt[0:1, 1, :], in0=lbt[0:1, 0, :], scalar1=-1.0, scalar2=None, op0=ALU.mult), reads=[b_m[0]], writes=[b_m[0]])
        S.op("sp", lambda e: e.dma_start(out=lnoml_d, in_=lbt[0:1, 1, :]), reads=[b_m[0]], writes=[b_lnd], dma=b_lnd)

        TA0 = 8192

        def TAf(off, n):
            return X[:, TA0 + off:TA0 + off + n]

        def TAb(off, n):
            return X[:, TA0 + off:TA0 + off + n].bitcast(BF16)

        oaT = X[:, 0:4096].bitcast(BF16).rearrange("p (c t) -> p c t", t=TT)
        obT = X[:, 4096:8192].bitcast(BF16).rearrange("p (c t) -> p c t", t=TT)
        xres = X[:].rearrange("p (t d) -> p t d", d=D)
        mergedT = M[:].bitcast(BF16).rearrange("p (c t) -> p c t", t=TT)
        hidT = mergedT

        def pipeline(items):
            n = len(items)
            ns = max(len(x) for x in items)
            for step in range(n + ns - 1):
                for st_ in reversed(range(ns)):
                    it = step - st_
                    if 0 <= it < n and st_ < len(items[it]):
                        items[it][st_]()

        def norm_phase(src_dram, tok0, gc, shc, xt_views, junk_view, tb, xres_src=False):
            fb = {}

            def front(t):
                i = t % 2
                xt = xt_views[i]
                bxt = tb[i]
                bjk = tb[2]
                ss = stat[:, i:i + 1]
                rs = stat[:, 2 + i:3 + i]
                if not xres_src:
                    S.op("sp", lambda e, xt=xt, t=t: e.dma_start(out=xt, in_=src_dram[tok0 + t * 128:tok0 + (t + 1) * 128, :]), writes=[bxt], dma=bxt)
                    srcap, srcb = xt, [bxt]
                else:
                    srcap, srcb = xres[:, t, :], b_xres[t]
                S.op("act", lambda e, srcap=srcap, ss=ss: e.activation(out=junk_view, in_=srcap, func=AF.Square, accum_out=ss), reads=srcb, writes=[bjk, b_stat[i]])
                rstd_ops(ss, rs, 1.0 / D, b_stat[i], b_stat[2 + i])
                fb[t] = lambda: S.op("dve", lambda e, xt=xt, srcap=srcap, rs=rs: e.tensor_scalar(out=xt, in0=srcap, scalar1=rs, scalar2=None, op0=ALU.mult),
                                     reads=srcb + [b_stat[2 + i]], writes=[bxt])

            def back(t):
                i = t % 2
                xt = xt_views[i]
                bxt = tb[i]
                for q4 in range(4):
                    k = nb()

                    def trs(pe, k=k, q4=q4, xt=xt):
                        ins = None
                        for j in range(4):
                            kc = q4 * 4 + j
                            ins = pe.transpose(PS[k][:, j * 128:(j + 1) * 128], xt[:, kc * 128:(kc + 1) * 128], IDF)
                        return ins
                    S.op("pe", trs, reads=[bxt, b_const], writes=bqs(k))
                    eng = "act" if q4 % 2 == 0 else "dve"

                    def ev(e, k=k, q4=q4, t=t, eng=eng):
                        ins = None
                        for j in range(4):
                            kc = q4 * 4 + j
                            o = HT[:, kc, t * 128:(t + 1) * 128]
                            src = PS[k][:, j * 128:(j + 1) * 128]
                            if eng == "act":
                                ins = e.activation(out=o, in_=src, func=AF.Identity, scale=gc[:, kc:kc + 1], bias=shc[:, kc:kc + 1])
                            else:
                                ins = e.tensor_scalar(out=o, in0=src, scalar1=gc[:, kc:kc + 1], scalar2=shc[:, kc:kc + 1], op0=ALU.mult, op1=ALU.add)
                        return ins
                    S.op(eng, ev, reads=bqs(k) + [b_small, b_mc, b_mc2], writes=[b_h[q4 * 4 + j][t] for j in range(4)])
            for k in range(NT + 1):
                if k < NT:
                    front(k)
                if k >= 1:
                    back(k - 1)
                if k < NT:
                    fb[k]()

        def hreads(tiles):
            return [b_h[kc][t] for kc in range(KC) for t in tiles]

        def hg_temps(slot):
            if slot == 0:
                base, off0, bl = X, TA0, b_ta
            else:
                base, off0, bl = M, 0, b_m

            def Ff(off, n):
                return base[:, off0 + off:off0 + off + n]

            def Fb(off, n):
                return base[:, off0 + off:off0 + off + n].bitcast(BF16)
            T = {}
            T["qT"] = Ff(0, 1024)
            T["kk_all"], T["lf_all"], T["lk_all"] = Ff(1024, 1024), Ff(2048, 1024), Ff(3072, 1024)
            T["kk"] = Ff(1024, 1024).rearrange("p (t f) -> p t f", f=128)
            T["lf"] = Ff(2048, 1024).rearrange("p (t f) -> p t f", f=128)
            T["lk"] = Ff(3072, 1024).rearrange("p (t f) -> p t f", f=128)
            T["vv"] = Fb(4096, 512).rearrange("p (t f) -> p t f", f=128)
            T["GG"] = Fb(4608, 512).rearrange("p (t f) -> p t f", f=128)
            T["stq"] = [Ff(5120, 512), Ff(5632, 512)]
            T["stg"] = [Ff(6144, 128), Ff(6272, 128)]
            T["stg2"] = [Ff(6400, 128), Ff(6528, 128)]
            T["AT"] = [Ff(6656, 128), Ff(6784, 128)]
            T["E3"] = [Ff(6912, 128), Ff(7040, 128)]
            T["qd"] = [Fb(7168, 64), Fb(7232, 64)]
            T["kd"] = [Fb(7296, 64), Fb(7360, 64)]
            T["kd2"] = [[Fb(7424, 64), Fb(6400, 64)], [Fb(7488, 64), Fb(6464, 64)]]
            T["scm"] = [Fb(7552, 64), Fb(7616, 64)]
            T["ofin"] = [Fb(7680, 64), Fb(7744, 64)]
            T["em"] = [Ff(7808, 4), Ff(7812, 4)]
            T["Ssc"] = [[Fb(7824, 64), Fb(7888, 64)], [Fb(7952, 64), Fb(8016, 64)]]
            T["osq"] = Fb(8080, 64)
            B = {}
            B["qT"], B["kk"], B["lf"], B["lk"], B["v"], B["G"] = (bl[i] for i in range(6))
            B["stq"] = [bl[6], bl[7]]
            B["stg"] = [bl[8], bl[9]]
            B["stg2"] = [bl[10], bl[11]]
            B["AT"] = [bl[12], bl[13]]
            B["E3"] = [bl[14], bl[15]]
            B["qd"] = [bl[16], bl[17]]
            B["kd"] = [bl[18], bl[19]]
            B["kd2"] = [bl[20], bl[21]]
            B["scm"] = [bl[22], bl[23]]
            B["ofin"] = [bl[24], bl[25]]
            B["em"] = [bl[26], bl[27]]
            B["Ssc"] = [[bl[28], bl[29]], [bl[30], bl[31]]]
            B["osq"] = bl[32]
            T["ss"] = [stat[:, 8 + 4 * slot + i:9 + 4 * slot + i] for i in range(2)]
            T["rs"] = [stat[:, 10 + 4 * slot + i:11 + 4 * slot + i] for i in range(2)]
            B["ss"] = [b_stat[4 + 4 * slot + i] for i in range(2)]
            B["rs"] = [b_stat[6 + 4 * slot + i] for i in range(2)]
            return T, B

        HG = [hg_temps(0), hg_temps(1)]

        def hg_temps_pf(slot):
            if slot < 2:
                base, off0, bl = X, TA0 + (slot % 2) * 4096, b_ta
            else:
                base, off0, bl = M, (slot % 2) * 4096, b_m
            nbufs = [S.buf("pf") for _ in range(12)]
            bl.extend(nbufs)

            def Ff(off, n):
                return base[:, off0 + off:off0 + off + n]

            def Fb(off, n):
                return base[:, off0 + off:off0 + off + n].bitcast(BF16)
            T = {n_: None for n_ in ("qT", "GG", "AT", "qd", "kd", "scm", "ofin", "Ssc")}
            T["kk_all"], T["lf_all"], T["lk_all"] = Ff(0, 1024), Ff(1024, 1024), Ff(2048, 1024)
            T["kk"] = Ff(0, 1024).rearrange("p (t f) -> p t f", f=128)
            T["lf"] = Ff(1024, 1024).rearrange("p (t f) -> p t f", f=128)
            T["lk"] = Ff(2048, 1024).rearrange("p (t f) -> p t f", f=128)
            T["vv"] = Fb(3072, 512).rearrange("p (t f) -> p t f", f=128)
            T["E3"] = [Ff(3584, 128), Ff(3712, 128)]
            T["kd2"] = [[Fb(3840, 64), Fb(3904, 64)], [Fb(3968, 64), Fb(4032, 64)]]
            T["em"] = [stat[:, 16 + slot * 8 + i * 4:20 + slot * 8 + i * 4] for i in range(2)]
            B = {"kk": nbufs[0], "lf": nbufs[1], "lk": nbufs[2], "v": nbufs[3], "E3": [nbufs[4], nbufs[5]],
                 "kd2": [nbufs[6], nbufs[7]], "em": [nbufs[8], nbufs[9]]}
            return T, B

        HGP = [hg_temps_pf(i) for i in range(4)]

        def hgrn_gen(j, own, slot, extra=None, wl=None):
            T, B = HG[slot] if own else HGP[slot]
            qT, kk, lf, lk, vv, GG = T["qT"], T["kk"], T["lf"], T["lk"], T["vv"], T["GG"]
            def load_head(jj):
                if own:
                    wv_, wb_ = wload([(w_in[:, C_QA + jj * 128:C_QA + (jj + 1) * 128], 0),
                                      (w_in[:, C_FA + jj * 128:C_FA + (jj + 1) * 128], 128),
                                      (w_in[:, C_IA + jj * 128:C_IA + (jj + 1) * 128], 256),
                                      (w_in[:, C_GA + jj * 128:C_GA + (jj + 1) * 128], 384)], 512)
                else:
                    wv_, wb_ = wload([(w_in[:, C_FA + jj * 128:C_FA + (jj + 1) * 128], 0),
                                      (w_in[:, C_IA + jj * 128:C_IA + (jj + 1) * 128], 128)], 256)
                wpin(wb_)
                return wv_, wb_
            if j not in wl:
                wl[j] = load_head(j)
            wv, wbuf = wl[j]
            if j + 1 < NH and (j + 1) not in wl:
                wl[j + 1] = load_head(j + 1)
            fo, nfi = (128, 384) if own else (0, 256)
            if own:
                for hf in range(2):
                    k = nb(pin=True)
                    mm_group(k, 0, [(wv[:, kc, 0:128], HT[:, kc, hf * 512:(hf + 1) * 512]) for kc in range(KC)],
                             [wbuf] + hreads(range(hf * 4, hf * 4 + 4)))
                    yield
                    silu_chain(k, 0, 512, T["stq"][hf], B["stq"][hf])
                    S.op("dve", lambda e, k=k, hf=hf: e.tensor_tensor(out=qT[:, hf * 512:(hf + 1) * 512], in0=PS[k][:, 0:512], in1=T["stq"][hf], op=ALU.mult),
                         reads=bqs(k) + [B["stq"][hf]], writes=[B["qT"]])
                    unpin(k)
                    yield
            for t in range(NT):
                i = t % 2
                k = nb(pin=True)
                mm_group(k, 0, [(HT[:, kc, t * 128:(t + 1) * 128], wv[:, kc, fo:fo + nfi]) for kc in range(KC)],
                         [wbuf] + hreads([t]))
                yield
                S.op("act", lambda e, k=k, t=t: e.activation(out=kk[:, t, :], in_=PS[k][:, 0:128], func=AF.Exp), reads=bqs(k), writes=[B["kk"]])
                S.op("dve", lambda e, k=k, t=t: e.tensor_copy(out=vv[:, t, :], in_=PS[k][:, 128:256]), reads=bqs(k), writes=[B["v"]])
                if own:
                    silu_chain(k, 256, 128, T["stg"][i], B["stg"][i])
                    S.op("dve", lambda e, k=k, i=i: e.tensor_tensor(out=T["stg"][i], in0=PS[k][:, 256:384], in1=T["stg"][i], op=ALU.mult),
                         reads=bqs(k) + [B["stg"][i]], writes=[B["stg"][i]])
                    S.op("dve", lambda e, t=t, i=i: e.tensor_tensor(out=GG[:, t, :], in0=T["stg"][i], in1=OG[:, j * 128:(j + 1) * 128], op=ALU.mult),
                         reads=[B["stg"][i], b_p2a], writes=[B["G"]])
                unpin(k)
                yield
            wunpin(wbuf)
            S.op("act", lambda e: e.activation(out=T["kk_all"], in_=T["kk_all"], func=AF.Ln, bias=1.0), reads=[B["kk"]], writes=[B["kk"]])
            for hh in range(2):
                S.op("dve", lambda e, hh=hh: e.tensor_tensor(out=lk[:, hh * 4:(hh + 1) * 4, :], in0=LNOML[:, j * 128:(j + 1) * 128].unsqueeze(1).broadcast_to([128, 4, 128]),
                                                        in1=kk[:, hh * 4:(hh + 1) * 4, :], op=ALU.subtract),
                     reads=[B["kk"], b_p2a], writes=[B["lk"]])
            S.op("act", lambda e: e.activation(out=T["kk_all"], in_=T["lk_all"], func=AF.Exp), reads=[B["lk"]], writes=[B["kk"]])
            S.op("act", lambda e: e.activation(out=T["lf_all"], in_=T["kk_all"], func=AF.Ln, scale=-1.0, bias=1.0), reads=[B["kk"]], writes=[B["lf"]])
            yield "spawn"
            if extra is not None:
                extra()
                yield
            STj = ST[:, j, :]
            AT, E3, qd, kd, kd2, scm, ofin, em, Ssc = (T[n_] for n_ in ("AT", "E3", "qd", "kd", "kd2", "scm", "ofin", "em", "Ssc"))

            def front(t):
                i = t % 2
                kx = nb(pin=True)
                ky = nb(pin=True) if own else kx
                dcol = (256, 384) if own else (0, 256)
                if own:
                    def cum(pe, kx=kx, t=t):
                        pe.matmul(PS[kx][:, 0:132], lhsT=lf[:, t, :], rhs=MA, start=True, stop=True)
                        pe.matmul(PS[kx][:, 256:384], lhsT=lk[:, t, :], rhs=IDF, start=True, stop=False)
                        pe.matmul(PS[kx][:, 256:384], lhsT=lf[:, t, :], rhs=NM1, start=False, stop=True)
                        return pe.matmul(PS[kx][:, 384:512], lhsT=M2c, rhs=lf[:, t, :], start=True, stop=True)
                    S.op("pe", cum, reads=[B["lf"], B["lk"], b_const], writes=bqs(kx))
                    yield
                    S.op("act", lambda e: e.activation(out=AT[i], in_=PS[kx][:, 0:128], func=AF.Exp), reads=bqs(kx), writes=[B["AT"][i]])
                    S.op("act", lambda e: e.activation(out=em[i], in_=PS[kx][:, 128:132], func=AF.Exp), reads=bqs(kx), writes=[B["em"][i]])
                    S.op("act", lambda e: e.activation(out=kd[i], in_=PS[kx][:, 256:384], func=AF.Exp), reads=bqs(kx), writes=[B["kd"][i]])
                else:
                    def cum(pe, kx=kx, t=t):
                        pe.matmul(PS[kx][:, 128:132], lhsT=lf[:, t, :], rhs=MA[:, 128:132], start=True, stop=True)
                        return pe.matmul(PS[kx][:, 384:512], lhsT=M2c, rhs=lf[:, t, :], start=True, stop=True)
                    S.op("pe", cum, reads=[B["lf"], b_const], writes=bqs(kx))
                    yield
                    S.op("act", lambda e: e.activation(out=em[i], in_=PS[kx][:, 128:132], func=AF.Exp), reads=bqs(kx), writes=[B["em"][i]])
                S.op("act", lambda e: e.activation(out=E3[i], in_=PS[kx][:, 384:512], func=AF.Exp), reads=bqs(kx), writes=[B["E3"][i]])
                yield
                for c in range(2):
                    S.op("dve", lambda e, c=c: e.scalar_tensor_tensor(out=kd2[i][c], in0=E3[i], scalar=RMc[:, c:c + 1], in1=kk[:, t, :], op0=ALU.mult, op1=ALU.mult),
                         reads=[B["E3"][i], B["kk"], b_const], writes=[B["kd2"][i]])
                if own:
                    S.op("dve", lambda e: e.tensor_tensor(out=qd[i], in0=qT[:, t * 128:(t + 1) * 128], in1=AT[i], op=ALU.mult), reads=[B["qT"], B["AT"][i]], writes=[B["qd"][i]])
                    yield
                    S.op("pe", lambda pe: pe.matmul(PS[ky][:, 0:128], lhsT=kd[i], rhs=qd[i], start=True, stop=True),
                         reads=[B["kd"][i], B["qd"][i]], writes=bqs(ky))
                    yield
                    S.op("dve", lambda e: e.tensor_tensor(out=scm[i], in0=PS[ky][:, 0:128], in1=MASKH, op=ALU.mult), reads=bqs(ky) + [b_const], writes=[B["scm"][i]])
                yield

                def dst(pe):
                    pe.matmul(PS[ky][:, dcol[0]:dcol[0] + 128], lhsT=kd2[i][0], rhs=vv[:, t, :], start=True, stop=True)
                    return pe.matmul(PS[ky][:, dcol[1]:dcol[1] + 128], lhsT=kd2[i][1], rhs=vv[:, t, :], start=True, stop=True)
                S.op("pe", dst, reads=[B["kd2"][i], B["v"]], writes=bqs(ky))
                if own:
                    unpin(kx)
                yield
                banks[t] = (kx, ky, dcol)

            def back(t):
                i = t % 2
                kx, ky, dcol = banks[t]
                dsrc = [(PS[ky][:, dcol[0]:dcol[0] + 128], bqs(ky)), (PS[ky][:, dcol[1]:dcol[1] + 128], bqs(ky))]
                for c in range(2):
                    if own:
                        S.op("act", lambda e, c=c: e.activation(out=Ssc[i][c], in_=STj, func=AF.Identity, scale=em[i][:, c:c + 1]),
                             reads=[b_ST[j], B["em"][i]], writes=[B["Ssc"][i][c]])
                    S.op("dve", lambda e, c=c: e.scalar_tensor_tensor(out=STj, in0=STj, scalar=em[i][:, 2 + c:3 + c], in1=dsrc[c][0], op0=ALU.mult, op1=ALU.add),
                         reads=[b_ST[j], B["em"][i]] + dsrc[c][1], writes=[b_ST[j]])
                    yield
                if own:
                    def omm(pe):
                        pe.matmul(PS[ky][:, 128:256], lhsT=scm[i], rhs=vv[:, t, :], start=True, stop=False)
                        pe.matmul(PS[ky][0:64, 128:256], lhsT=qd[i][:, 0:64], rhs=Ssc[i][0], start=False, stop=True)
                        return pe.matmul(PS[ky][64:128, 128:256], lhsT=qd[i][:, 64:128], rhs=Ssc[i][1], start=False, stop=True)
                    S.op("pe", omm, reads=[B["scm"][i], B["v"], B["qd"][i], B["Ssc"][i][0], B["Ssc"][i][1]], writes=bqs(ky))
                    yield
                    ss, rs = T["ss"][i], T["rs"][i]
                    S.op("act", lambda e: e.activation(out=T["osq"], in_=PS[ky][:, 128:256], func=AF.Square, accum_out=ss), reads=bqs(ky), writes=[B["osq"], B["ss"][i]])
                    rstd_ops(ss, rs, 1.0 / 128, B["ss"][i], B["rs"][i])
                    yield
                    S.op("dve", lambda e: e.scalar_tensor_tensor(out=ofin[i], in0=PS[ky][:, 128:256], scalar=rs, in1=GG[:, t, :], op0=ALU.mult, op1=ALU.mult),
                         reads=bqs(ky) + [B["rs"][i], B["G"]], writes=[B["ofin"][i]])
                    yield
                    unpin(ky)
                    kz = nb(pin=True)
                    S.op("pe", lambda pe: pe.transpose(PSB[kz][:, 0:128], ofin[i], IDB), reads=[B["ofin"][i], b_const], writes=bqs(kz))
                    yield
                    S.op("act", lambda e: e.activation(out=oaT[:, j, t * 128:(t + 1) * 128], in_=PSB[kz][:, 0:128], func=AF.Copy), reads=bqs(kz), writes=[b_oaT[j][t]])
                    unpin(kz)
                    yield
                else:
                    unpin(ky)

            banks = {}
            if own:
                yield from front(0)
                for t in range(NT):
                    if t + 1 < NT:
                        yield from front(t + 1)
                    yield from back(t)
            else:
                for t in range(NT):
                    yield from front(t)
                    yield from back(t)

        def run_interleaved(gens, max_active=2):
            pending = list(gens)
            active = []
            credit = 1
            while pending or active:
                while pending and len(active) < max_active and (credit > 0 or not active):
                    active.append(pending.pop(0))
                    credit = max(0, credit - 1)
                for g in list(active):
                    try:
                        r = next(g)
                        if r == "spawn":
                            credit += 1
                    except StopIteration:
                        active.remove(g)

        qhT = TAb(0, 4096).rearrange("p (c t) -> p c t", t=TT)
        khT = TAb(4096, 2048).rearrange("p (c t) -> p c t", t=TT)
        VX = TAb(6144, 1040).rearrange("p (n h d) -> p n h d", h=4, d=65)
        ssq8 = TAf(7184, 8)
        rs8 = TAf(7192, 8)
        den4 = [TAf(7200, 4), TAf(7204, 4)]
        BC_qhT = [[b_ta[c * 8 + t] for t in range(8)] for c in range(2)]
        BC_khT = [b_ta[16 + t] for t in range(8)]
        BC_VX = [b_ta[24 + t] for t in range(8)]
        BC_s8, BC_r8 = b_ta[32], b_ta[33]
        BC_den = [b_ta[34], b_ta[35]]
        sqf = M[:, 0:512]
        qn = M[:, 512:1024]
        qhb = M[:, 1024:1280].bitcast(BF16)
        kdup = M[:, 1280:1536].bitcast(BF16).rearrange("p (h r d) -> p h r d", r=2, d=64)
        EX = [M[:, 1536:1792].bitcast(BF16), M[:, 1792:2048].bitcast(BF16)]
        PT = [[M[:, 2048 + (i * 2 + kb) * 256:2048 + (i * 2 + kb + 1) * 256].bitcast(BF16) for kb in range(2)] for i in range(2)]
        obt = [M[:, 3072:3584].bitcast(BF16), M[:, 3584:4096].bitcast(BF16)]
        BM_sqf, BM_qn, BM_qhb, BM_kdup = b_m[0], b_m[1], b_m[2], b_m[3]
        BM_EX = [b_m[4], b_m[5]]
        BM_PT = [[b_m[6], b_m[7]], [b_m[8], b_m[9]]]
        BM_obt = [b_m[10], b_m[11]]

        def head_norm(k, c0, nh, gain_bc, out_bf, bout):
            n = nh * 64
            src = PS[k][:, c0:c0 + n]
            rb = bqs(k, c0, c0 + n)
            S.op("act", lambda e: e.activation(out=sqf[:, 0:n], in_=src, func=AF.Square), reads=rb, writes=[BM_sqf])
            S.op("dve", lambda e: e.tensor_reduce(out=ssq8[:, 0:nh], in_=sqf[:, 0:n].rearrange("p (h d) -> p h d", d=64), axis=AX.X, op=ALU.add), reads=[BM_sqf], writes=[BC_s8])
            rstd_ops(ssq8[:, 0:nh], rs8[:, 0:nh], 1.0 / 64, BC_s8, BC_r8)
            S.op("dve", lambda e: e.tensor_tensor(out=qn[:, 0:n].rearrange("p (h d) -> p h d", d=64), in0=src.rearrange("p (h d) -> p h d", d=64),
                                                  in1=rs8[:, 0:nh].unsqueeze(2).broadcast_to([128, nh, 64]), op=ALU.mult), reads=rb + [BC_r8], writes=[BM_qn])
            for (oap, extra) in out_bf:
                S.op("dve", lambda e, oap=oap: e.tensor_tensor(out=oap, in0=qn[:, 0:n].rearrange("p (h d) -> p h d", d=64),
                                                                in1=gain_bc.broadcast_to([128, nh, 64]), op=ALU.mult), reads=[BM_qn, b_small, b_const], writes=[bout])

        def kv_stages(wv, wbuf, t, dstK, dstV, bK, bV):
            st_ = {}

            def A():
                st_["k"] = nb()
                mm_group(st_["k"], 0, [(HT[:, kc, t * 128:(t + 1) * 128], wv[:, kc, :]) for kc in range(KC)], [wbuf] + hreads([t]))

            def Bs():
                k = st_["k"]
                head_norm(k, 0, 4, kgb.unsqueeze(1), [(kdup[:, :, 0, :], None), (kdup[:, :, 1, :], None)], BM_kdup)
                S.op("dve", lambda e: e.tensor_copy(out=dstV, in_=PS[k][:, 256:512].rearrange("p (h d) -> p h d", d=64)), reads=bqs(k), writes=[bV])

            def Cs():
                kz = nb()

                def trs(pe):
                    ins = None
                    for hk in range(4):
                        ins = pe.transpose(PSB[kz][:, hk * 128:(hk + 1) * 128], kdup[:, hk, :, :].rearrange("p r d -> p (r d)"), IDB)
                    return ins
                S.op("pe", trs, reads=[BM_kdup, b_const], writes=bqs(kz))
                S.op("act", lambda e: e.activation(out=dstK, in_=PSB[kz][:, 0:512].rearrange("p (h t) -> p h t", t=128), func=AF.Copy), reads=bqs(kz), writes=[bK])
            return [A, Bs, Cs]

        def kv_tile(wv, wbuf, t, dstK, dstV, bK, bV):
            for f in kv_stages(wv, wbuf, t, dstK, dstV, bK, bV):
                f()

        def q_stages(wv, wbuf, pc, t):
            st_ = {}

            def A():
                st_["k"] = nb()
                mm_group(st_["k"], 0, [(HT[:, kc, t * 128:(t + 1) * 128], wv[:, kc, :]) for kc in range(KC)], [wbuf] + hreads([t]))

            def Bs():
                head_norm(st_["k"], 0, 8, qgb, [(qhb.rearrange("p (h d) -> p h d", d=64), None)], BM_qhb)

            def Cs():
                kz = nb()

                def trs(pe):
                    ins = None
                    for c in range(4):
                        ins = pe.transpose(PSB[kz][:, c * 128:(c + 1) * 128], qhb[:, c * 128:(c + 1) * 128], IDB)
                    return ins
                S.op("pe", trs, reads=[BM_qhb, b_const], writes=bqs(kz))
                S.op("act", lambda e: e.activation(out=qhT[:, pc * 4:(pc + 1) * 4, t * 128:(t + 1) * 128],
                                                   in_=PSB[kz][:, 0:512].rearrange("p (c t) -> p c t", t=128), func=AF.Copy),
                     reads=bqs(kz), writes=[BC_qhT[pc][t]])
            return [A, Bs, Cs]

        def swa_phase(first_group):
            S.op("dve", lambda e: e.memset(VX[:, :, :, 64:65], 1.0), writes=BC_VX)
            items = []
            wq = [wload([(w_in[:, C_QB + pc * 512:C_QB + (pc + 1) * 512], 0)], 512) for pc in range(2)]
            wkv = wload([(w_in[:, C_KB:C_KB + 512], 0)], 512)
            for pc in range(2):
                for t in range(NT):
                    items.append(q_stages(wq[pc][0], wq[pc][1], pc, t))
            for t in range(NT):
                items.append(kv_stages(wkv[0], wkv[1], t, khT[:, :, t * 128:(t + 1) * 128], VX[:, t, :, 0:64], BC_khT[t], BC_VX[t]))
            pipeline(items)

            def blk_stages(n, hk):
                i3 = n % 2
                i2 = hk % 2

                def Ss():
                    for kb in range(2):
                        ksa, ksb = nb(), nb()
                        halo = (kb == 0 and n == 0)

                        def sc(pe, ksa=ksa, ksb=ksb, kb=kb, halo=halo):
                            ins = None
                            for g in range(4):
                                base = (g % 2) * 64
                                ch = 2 * hk + g // 2
                                if halo:
                                    l = KH[base:base + 64, hk, :]
                                else:
                                    blk = n - 1 + kb
                                    l = khT[base:base + 64, hk, blk * 128:(blk + 1) * 128]
                                r = qhT[base:base + 64, ch, n * 128:(n + 1) * 128]
                                kk_ = ksa if g % 2 == 0 else ksb
                                ins = pe.matmul(PS[kk_][:, (g // 2) * 128:(g // 2 + 1) * 128], lhsT=l, rhs=r, start=True, stop=True)
                            return ins
                        rd = [BC_qhT[(2 * hk) // 4][n]]
                        rd.append(b_halo if halo else BC_khT[n - 1 + kb])
                        S.op("pe", sc, reads=rd, writes=bqs(ksa) + bqs(ksb))
                        exv = EX[kb].rearrange("p (a b q) -> p a b q", b=2, q=128)

                        def exf(e, ksa=ksa, ksb=ksb, exv=exv):
                            e.activation(out=exv[:, :, 0, :], in_=PS[ksa][:, 0:256].rearrange("p (a q) -> p a q", q=128), func=AF.Exp)
                            return e.activation(out=exv[:, :, 1, :], in_=PS[ksb][:, 0:256].rearrange("p (a q) -> p a q", q=128), func=AF.Exp)
                        S.op("act", exf, reads=bqs(ksa) + bqs(ksb), writes=[BM_EX[kb]])
                        if kb == 1:
                            mk = MCm
                        elif halo and first_group:
                            mk = MP0[:]
                        else:
                            mk = MPm
                        S.op("dve", lambda e, kb=kb, mk=mk: e.tensor_tensor(out=PT[i2][kb].rearrange("p (g q) -> p g q", q=128), in0=EX[kb].rearrange("p (g q) -> p g q", q=128),
                                                                         in1=mk.unsqueeze(1).broadcast_to([128, 4, 128]), op=ALU.mult),
                             reads=[BM_EX[kb], b_const], writes=[BM_PT[i2][kb]])

                def Ps():
                    ko = nb()

                    def pv(pe):
                        ins = None
                        for g in range(4):
                            vp = VH[:, hk, :] if n == 0 else VX[:, n - 1, hk, :]
                            pe.matmul(PS[ko][:, g * 65:(g + 1) * 65], lhsT=PT[i2][0][:, g * 128:(g + 1) * 128], rhs=vp, start=True, stop=False)
                            ins = pe.matmul(PS[ko][:, g * 65:(g + 1) * 65], lhsT=PT[i2][1][:, g * 128:(g + 1) * 128], rhs=VX[:, n, hk, :], start=False, stop=True)
                        return ins
                    S.op("pe", pv, reads=[BM_PT[i2][0], BM_PT[i2][1], BC_VX[n], (b_halo if n == 0 else BC_VX[n - 1])], writes=bqs(ko))
                    ov = PS[ko][:, 0:260].rearrange("p (g c) -> p g c", c=65)
                    S.op("dve", lambda e: e.tensor_tensor(out=den4[i2].unsqueeze(2), in0=ov[:, :, 64:65], in1=esink[:, hk * 4:(hk + 1) * 4].unsqueeze(2), op=ALU.add),
                         reads=bqs(ko) + [b_small], writes=[BC_den[i2]])
                    S.op("dve", lambda e: e.reciprocal(out=den4[i2], in_=den4[i2]), reads=[BC_den[i2]], writes=[BC_den[i2]])
                    S.op("dve", lambda e: e.tensor_tensor(out=obt[i3][:, hk * 256:(hk + 1) * 256].rearrange("p (g d) -> p g d", d=64), in0=ov[:, :, 0:64],
                                                          in1=den4[i2].unsqueeze(2).broadcast_to([128, 4, 64]), op=ALU.mult),
                         reads=bqs(ko) + [BC_den[i2]], writes=[BM_obt[i3]])
                    if hk == 3:
                        kz = nb()

                        def trs(pe):
                            ins = None
                            for c in range(8):
                                ins = pe.transpose(PSB[kz][:, c * 128:(c + 1) * 128], obt[i3][:, c * 128:(c + 1) * 128], IDB)
                            return ins
                        S.op("pe", trs, reads=[BM_obt[i3], b_const], writes=bqs(kz))
                        S.op("act", lambda e: e.activation(out=obT[:, :, n * 128:(n + 1) * 128], in_=PSB[kz][:, 0:1024].rearrange("p (c t) -> p c t", t=128), func=AF.Copy),
                             reads=bqs(kz), writes=[b_obT[n]])
                return [Ss, Ps]
            pipeline([blk_stages(n, hk) for n in range(NT) for hk in range(4)])
            S.op("dve", lambda e: e.tensor_copy(out=KH[:], in_=khT[:, :, 7 * 128:8 * 128]), reads=[BC_khT[7]], writes=[b_halo])
            S.op("dve", lambda e: e.tensor_copy(out=VH[:, :, 0:64], in_=VX[:, 7, :, 0:64]), reads=[BC_VX[7]], writes=[b_halo])

        t1s = TAf(0, 4096).rearrange("p (c t) -> p c t", t=TT)
        sgst = [TAf(4096, 512), TAf(4608, 512)]
        t2st = [TAf(5120, 512), TAf(5632, 512)]
        BD_t1 = [[b_ta[c * 2 + hf] for hf in range(2)] for c in range(4)]
        BD_sg = [b_ta[8], b_ta[9]]
        BD_t2 = [b_ta[10], b_ta[11]]

        def wload2(src_g, src_y):
            i = wslot()
            vg = W[i][:, 0:4096].rearrange("p (a b) -> p a b", b=256)
            vy = W[i][:, 4096:6144].rearrange("p (a b) -> p a b", b=256)
            S.op("pool", lambda e: e.dma_start(out=vg, in_=src_g.rearrange("(kc p) n -> p kc n", p=128)), writes=[wb[i]], dma=wb[i])
            S.op("pool", lambda e: e.dma_start(out=vy, in_=src_y.rearrange("(kc p) n -> p kc n", p=128)), writes=[wb[i]], dma=wb[i])
            return vg, vy, wb[i]

        def merge_phase():
            cnt = 0
            for cq in range(4):
                for br in range(2):
                    wsrc = w_a if br == 0 else w_b
                    oT = oaT if br == 0 else obT
                    for h2 in range(2):
                        gcol = (C_GATEA if br == 0 else C_GATEB) + cq * 512 + h2 * 256
                        wg, wy, wbuf_ = wload2(w_in[:, gcol:gcol + 256], wsrc[:, cq * 512 + h2 * 256:cq * 512 + (h2 + 1) * 256])
                        for c2 in range(2):
                            c4 = h2 * 2 + c2
                            c = cq * 4 + c4
                            for hf in range(2):
                                i = cnt % 2
                                cnt += 1
                                kg_ = nb()
                                mm_group(kg_, 0, [(wg[:, kc, c2 * 128:(c2 + 1) * 128], HT[:, kc, hf * 512:(hf + 1) * 512]) for kc in range(KC)],
                                         [wbuf_] + hreads(range(hf * 4, hf * 4 + 4)))
                                ky_ = nb()
                                if br == 0:
                                    rdo = [b_oaT[kc][t] for kc in range(8) for t in range(hf * 4, hf * 4 + 4)]
                                else:
                                    rdo = [b_obT[t] for t in range(hf * 4, hf * 4 + 4)]
                                mm_group(ky_, 0, [(wy[:, kc, c2 * 128:(c2 + 1) * 128], oT[:, kc, hf * 512:(hf + 1) * 512]) for kc in range(8)], [wbuf_] + rdo)
                                S.op("act", lambda e, kg_=kg_, i=i: e.activation(out=sgst[i], in_=PS[kg_][:, 0:512], func=AF.Sigmoid), reads=bqs(kg_), writes=[BD_sg[i]])
                                if br == 0:
                                    S.op("dve", lambda e, ky_=ky_, i=i, c4=c4, hf=hf: e.tensor_tensor(out=t1s[:, c4, hf * 512:(hf + 1) * 512], in0=PS[ky_][:, 0:512], in1=sgst[i], op=ALU.mult),
                                         reads=bqs(ky_) + [BD_sg[i]], writes=[BD_t1[c4][hf]])
                                else:
                                    S.op("dve", lambda e, ky_=ky_, i=i: e.tensor_tensor(out=t2st[i], in0=PS[ky_][:, 0:512], in1=sgst[i], op=ALU.mult),
                                         reads=bqs(ky_) + [BD_sg[i]], writes=[BD_t2[i]])
                                    S.op("dve", lambda e, i=i, c=c, c4=c4, hf=hf: e.tensor_tensor(out=mergedT[:, c, hf * 512:(hf + 1) * 512], in0=t1s[:, c4, hf * 512:(hf + 1) * 512], in1=t2st[i], op=ALU.add),
                                         reads=[BD_t2[i], BD_t1[c4][hf]], writes=[b_mT[c][hf]])

        HTf = HT[:].rearrange("p a b -> p (a b)").bitcast(F32)
        GT1 = HTf[:, 0:2048]
        tmpE = [HTf[:, 2048:2560], HTf[:, 2560:3072]]

        def outproj_phase(tok0):
            for t in range(NT):
                S.op("sp", lambda e, t=t: e.dma_start(out=xres[:, t, :], in_=x_own[tok0 + t * 128:tok0 + (t + 1) * 128, :]), writes=b_xres[t], dma=b_xres[t][0])
            S.op("sp", lambda e: e.dma_start(out=GT1, in_=mod_d[0, 2 * D:3 * D].partition_broadcast(128)), reads=b_modd2p, writes=[b_hte[0]], dma=b_hte[0])
            cnt = 0
            for pc in range(4):
                wv, wbuf = wload([(w_out[:, pc * 512:(pc + 1) * 512], 0)], 512)
                for t in range(NT):
                    i = cnt % 2
                    cnt += 1
                    k = nb()
                    mm_group(k, 0, [(mergedT[:, kc, t * 128:(t + 1) * 128], wv[:, kc, :]) for kc in range(KC)],
                             [wbuf] + [b_mT[kc][t // 4] for kc in range(KC)])
                    S.op("dve", lambda e, k=k, i=i, pc=pc: e.tensor_tensor(out=tmpE[i], in0=PS[k][:, 0:512], in1=GT1[:, pc * 512:(pc + 1) * 512], op=ALU.mult),
                         reads=bqs(k) + [b_hte[0]], writes=[b_hte[1 + i]])
                    S.op("dve", lambda e, i=i, t=t, pc=pc: e.tensor_tensor(out=xres[:, t, pc * 512:(pc + 1) * 512], in0=xres[:, t, pc * 512:(pc + 1) * 512], in1=tmpE[i], op=ALU.add),
                         reads=[b_hte[1 + i], b_xres[t][0], b_xres[t][pc]], writes=[b_xres[t][pc]])

        rl = [STG[:, 0:512], STG[:, 512:1024]]
        tmpG = [STG[:, 1024:1536]]

        def mlp_phase(tok0):
            cnt = 0
            for hb in range(4):
                for w1p in range(4):
                    wv, wbuf = wload([(w1[:, hb * 2048 + w1p * 512:hb * 2048 + (w1p + 1) * 512], 0)], 512)
                    for c4 in range(4):
                        hc = w1p * 4 + c4
                        for hf in range(2):
                            i = cnt % 2
                            cnt += 1
                            k = nb()
                            mm_group(k, 0, [(wv[:, kc, c4 * 128:(c4 + 1) * 128], HT[:, kc, hf * 512:(hf + 1) * 512]) for kc in range(KC)],
                                     [wbuf] + hreads(range(hf * 4, hf * 4 + 4)))
                            S.op("act", lambda e, k=k, i=i: e.activation(out=rl[i], in_=PS[k][:, 0:512], func=AF.Relu), reads=bqs(k), writes=[b_stg[i]])
                            S.op("dve", lambda e, i=i, hc=hc, hf=hf: e.tensor_tensor(out=hidT[:, hc, hf * 512:(hf + 1) * 512], in0=rl[i], in1=rl[i], op=ALU.mult),
                                 reads=[b_stg[i]], writes=[b_hid[hc][hf]])
                for cq in range(4):
                    wv, wbuf = wload([(w2[hb * 2048:(hb + 1) * 2048, cq * 512:(cq + 1) * 512], 0)], 512)
                    for t in range(NT):
                        k = nb()
                        mm_group(k, 0, [(hidT[:, hc, t * 128:(t + 1) * 128], wv[:, hc, :]) for hc in range(16)],
                                 [wbuf] + [b_hid[hc][t // 4] for hc in range(16)])
                        S.op("dve", lambda e, k=k, cq=cq: e.tensor_tensor(out=tmpG[0], in0=PS[k][:, 0:512], in1=GT2[:, cq * 512:(cq + 1) * 512], op=ALU.mult),
                             reads=bqs(k) + [b_p2b], writes=[b_stg[2]])
                        S.op("dve", lambda e, t=t, cq=cq: e.tensor_tensor(out=xres[:, t, cq * 512:(cq + 1) * 512], in0=xres[:, t, cq * 512:(cq + 1) * 512], in1=tmpG[0], op=ALU.add),
                             reads=[b_stg[2], b_xres[t][cq]], writes=[b_xres[t][cq]])
            for t in range(NT):
                S.op("sp", lambda e, t=t: e.dma_start(out=out_d[tok0 + t * 128:tok0 + (t + 1) * 128, :], in_=xres[:, t, :]), reads=b_xres[t], writes=[b_out], dma=b_out)

        XT_A = [TAf(0, 2048), TAf(2048, 2048)]
        JK_A = TAb(4096, 1024)
        XT_F = [M[:, 0:2048], M[:, 2048:4096]]
        JK_F = M[:, 4096:5120].bitcast(BF16)
        all_x = [b_oaT, b_obT, b_xres, b_ta]
        all_m = [b_m, b_mT, b_hid]

        def load_p2a():
            S.op("sp", lambda e: e.dma_start(out=LNOML, in_=lnoml_d[0, :].partition_broadcast(128)), reads=[b_lnd], writes=[b_p2a], dma=b_p2a)
            S.op("sp", lambda e: e.dma_start(out=OG, in_=ogain_d[0, :].partition_broadcast(128)), writes=[b_p2a], dma=b_p2a)

        def program():
            stop = CFG.get('stop')
            if stop == 'setup':
                return
            fence([all_x, all_m, b_p2a, b_p2b])
            load_p2a()
            if CFG["prefix"]:
                for pg in range(2):
                    norm_phase(x_pre, pg * TT, g1c, sh1c, XT_A, JK_A, [b_ta[40], b_ta[41], b_ta[42]])
                    fence([b_ta, b_m])
                    wl_ = {}
                    run_interleaved([hgrn_gen(j, False, j % 4, extra=(ada_extra(pg, j) if pg == 0 else None), wl=wl_) for j in range(NH)], max_active=4)
                    if pg == 1:
                        wv, wbuf = wload([(w_in[:, C_KB:C_KB + 512], 0)], 512)
                        fence([b_ta, b_m])
                        kv_tile(wv, wbuf, 7, KH[:], VH[:, :, 0:64], b_halo, b_halo)
                    else:
                        load_mod2()
                    fence([b_ta, b_m])
                S.op("dve", lambda e: e.tensor_scalar(out=ST[:].rearrange("p a b -> p (a b)"), in0=ST[:].rearrange("p a b -> p (a b)"), scalar1=flagt, scalar2=None, op0=ALU.mult),
                     reads=b_ST + [b_const], writes=b_ST)
            if DBG:
                S.op("sp", lambda e: e.dma_start(out=dbg["st"], in_=ST[:]), reads=b_ST, writes=[b_dbg], dma=b_dbg)

            for g in range(NG):
                tok0 = g * TT
                norm_phase(x_own, tok0, g1c, sh1c, XT_A, JK_A, [b_ta[40], b_ta[41], b_ta[42]])
                if stop == 'normA':
                    S.op('sp', lambda e: e.dma_start(out=dbg['hT'], in_=HT[:]), reads=hreads(range(NT)), writes=[b_dbg], dma=b_dbg)
                    return
                if DBG and g == 0:
                    S.op("sp", lambda e: e.dma_start(out=dbg["hT"], in_=HT[:]), reads=hreads(range(NT)), writes=[b_dbg], dma=b_dbg)
                fence([b_ta, b_m])
                wl_ = {}
                run_interleaved([hgrn_gen(j, True, j % 2, wl=wl_) for j in range(CFG.get('nheads', NH))])
                if stop == 'hgrn':
                    S.op('sp', lambda e: e.dma_start(out=dbg['oaT'], in_=oaT), reads=flat(b_oaT), writes=[b_dbg], dma=b_dbg)
                    return
                fence([b_ta, all_m])
                swa_phase(first_group=(g == 0))
                if stop == 'swa':
                    S.op('sp', lambda e: e.dma_start(out=dbg['obT'], in_=obT), reads=flat(b_obT), writes=[b_dbg], dma=b_dbg)
                    return
                if DBG and g == 0:
                    S.op("sp", lambda e: e.dma_start(out=dbg["oaT"], in_=oaT), reads=flat(b_oaT), writes=[b_dbg], dma=b_dbg)
                    S.op("sp", lambda e: e.dma_start(out=dbg["obT"], in_=obT), reads=flat(b_obT), writes=[b_dbg], dma=b_dbg)
                fence([b_ta, all_m])
                merge_phase()
                if DBG and g == 0:
                    S.op("sp", lambda e: e.dma_start(out=dbg["mT"], in_=mergedT), reads=flat(b_mT), writes=[b_dbg], dma=b_dbg)
                if stop == 'merge':
                    return
                fence([all_x, b_h, b_hte])
                outproj_phase(tok0)
                if DBG and g == 0:
                    S.op("sp", lambda e: e.dma_start(out=dbg["x1"], in_=xres), reads=flat(b_xres), writes=[b_dbg], dma=b_dbg)
                if stop == 'outproj':
                    return
                fence([all_m, b_h, b_hte, b_p2a, b_p2b, b_stg])
                S.op("sp", lambda e: e.dma_start(out=GT2, in_=mod_d[0, 5 * D:6 * D].partition_broadcast(128)), reads=b_modd2p, writes=[b_p2b], dma=b_p2b)
                norm_phase(None, 0, g2c, sh2c, XT_F, JK_F, [b_m[40], b_m[41], b_m[42]], xres_src=True)
                fence([all_m])
                mlp_phase(tok0)
                if g + 1 < NG:
                    fence([all_x, all_m, b_p2a, b_p2b, b_stg])
                    load_p2a()


        program()
        fw = ([b_out] if b_out.dma_cnt else []) + ([b_dbg] if (DBG and b_dbg.dma_cnt) else [])
        S.emit(final_wait_bufs=fw)
    return nc


def make_in_maps(inp):
    x = np.asarray(inp["x"], np.float32)
    c = np.asarray(inp["c"], np.float32)
    cf, cb, MP = _consts()
    f32 = lambda a: np.ascontiguousarray(np.asarray(a, np.float32))
    shared = {
        "w_ada": f32(inp["w_ada"][0]), "b_ada": f32(inp["b_ada"][0]).reshape(1, -1),
        "n1c": f32(np.asarray(inp["norm1_gain"][0]).reshape(16, 128).T),
        "n2c": f32(np.asarray(inp["norm2_gain"][0]).reshape(16, 128).T),
        "w_in": f32(inp["w_in"][0]), "lb_logits": f32(inp["lb_logits"]),
        "hgrn_o_gain": f32(inp["hgrn_o_gain"][0]).reshape(1, -1),
        "q_norm_gain": f32(inp["q_norm_gain"][0]).reshape(1, -1),
        "k_norm_gain": f32(inp["k_norm_gain"][0]).reshape(1, -1),
        "sinks": f32(inp["sinks"][0]).reshape(1, -1),
        "w_branch_a": f32(inp["w_branch_a"][0]), "w_branch_b": f32(inp["w_branch_b"][0]),
        "w_out": f32(inp["w_out"][0]), "w_mlp_in": f32(inp["w_mlp_in"][0]), "w_mlp_out": f32(inp["w_mlp_out"][0]),
        "cf": cf, "cb": cb,
    }
    zeros_x = np.zeros((TCORE, D), np.float32)
    maps = []
    for core in range(8):
        b, hf = core // 2, core % 2
        m = dict(shared)
        m["x_own"] = np.ascontiguousarray(x[b, hf * TCORE:(hf + 1) * TCORE])
        m["x_pre"] = np.ascontiguousarray(x[b, 0:TCORE]) if hf == 1 else zeros_x
        m["flag"] = np.full((128, 1), float(hf), np.float32)
        m["c_col"] = np.ascontiguousarray(c[b].reshape(16, 128).T)
        m["mp0"] = MP if hf == 1 else np.zeros_like(MP)
        maps.append(m)
    return maps


def kernel(**inp):
    nc = build_program()
    maps = make_in_maps(inp)
    res = run_bass_kernel_spmd(nc, maps, core_ids=list(range(8)))
    out = np.zeros((4, 4096, D), np.float32)
    for core in range(8):
        b, hf = core // 2, core % 2
        out[b, hf * TCORE:(hf + 1) * TCORE] = np.asarray(res.results[core]["out"], np.float32)
    kernel.last_results = res
    return out
```
